# Optimizing a Trainium2 kernel written in Bass

```python
import jax, jax.numpy as jnp
from jax import lax
import numpy as np

D_MODEL = 4096
BATCH = 8
SEQ = 2048
DEPTH = 2

N_MIXERS = 2
POOL_WINDOWS = (2, 4, 8, 16)
N_POOL_GROUPS = len(POOL_WINDOWS)
POOL_GROUP = D_MODEL // N_POOL_GROUPS
GLA_HEADS = 4
GLA_KEY_DIM = D_MODEL // 2
GLA_VAL_DIM = D_MODEL
GLA_HEAD_K = GLA_KEY_DIM // GLA_HEADS
GLA_HEAD_V = GLA_VAL_DIM // GLA_HEADS
GATE_RANK = 16
GATE_TAU = 16.0
CHUNK = 64
D_FF = 4 * D_MODEL
EPS = 1e-6
N_POOL_LAYERS = (DEPTH + 1) // 2
N_GLA_LAYERS = DEPTH // 2
PROJ_WIDTH = 2 * GLA_KEY_DIM + 2 * GLA_VAL_DIM + 2 * GATE_RANK

kernel_name = "hybrid_pool_gla_encoder"


def rmsnorm(x, g):
    xf = x.astype(jnp.float32)
    y = xf * lax.rsqrt(jnp.mean(xf * xf, axis=-1, keepdims=True) + EPS)
    return (y * g.astype(jnp.float32)).astype(x.dtype)


def pool_mixer(h, w_group, scale):
    B, S, D = h.shape
    hf = h.astype(jnp.float32)
    csum = jnp.concatenate([jnp.zeros((B, 1, D), jnp.float32), jnp.cumsum(hf, axis=1)], axis=1)
    pos = jnp.arange(S)
    diffs = []
    for g, w in enumerate(POOL_WINDOWS):
        lo = jnp.clip(pos - w // 2, 0, S)
        hi = jnp.clip(pos + w // 2, 0, S)
        sl = slice(g * POOL_GROUP, (g + 1) * POOL_GROUP)
        cs = csum[..., sl]
        count = (hi - lo).astype(jnp.float32)[None, :, None]
        mean = (cs[:, hi] - cs[:, lo]) / count
        diffs.append(mean - hf[..., sl])
    d = jnp.stack(diffs, axis=2).astype(h.dtype)
    y = jnp.einsum('bsgc,gcd->bsgd', d, w_group).reshape(B, S, D)
    return y * scale


def gla_chunked(q, k, v, log_a):
    B, S, H, dk = q.shape
    dv = v.shape[-1]
    n_chunks = S // CHUNK

    def to_chunks(t):
        return t.astype(jnp.float32).reshape(B, n_chunks, CHUNK, H, -1).transpose(1, 0, 3, 2, 4)

    qc, kc, vc, gc = to_chunks(q), to_chunks(k), to_chunks(v), to_chunks(log_a)
    b = jnp.cumsum(gc, axis=3)
    b_last = b[..., CHUNK - 1:, :]
    b_mid = b[..., CHUNK // 2 - 1:CHUNK // 2, :]
    scores = jnp.einsum('nbhik,nbhjk->nbhij', qc * jnp.exp(b - b_mid), kc * jnp.exp(b_mid - b))
    mask = jnp.tril(jnp.ones((CHUNK, CHUNK), dtype=bool))
    scores = jnp.where(mask, scores, 0.0)
    o_intra = jnp.einsum('nbhij,nbhjv->nbhiv', scores, vc)
    q_inter = qc * jnp.exp(b)
    k_state = kc * jnp.exp(b_last - b)
    a_chunk = jnp.exp(b_last[..., 0, :])

    def step(state, inp):
        qi, ki, vi, ai = inp
        o = jnp.einsum('bhik,bhkv->bhiv', qi, state)
        state = ai[..., None] * state + jnp.einsum('bhjk,bhjv->bhkv', ki, vi)
        return state, o

    state0 = jnp.zeros((B, H, dk, dv), jnp.float32)
    _, o_inter = lax.scan(step, state0, (q_inter, k_state, vc, a_chunk))
    o = o_intra + o_inter
    return o.transpose(1, 0, 3, 2, 4).reshape(B, S, H, dv)


def gla_mixer(h, w_in, w_up_f, b_up_f, w_up_b, b_up_b, g_norm, w_out):
    B, S, D = h.shape
    p = h @ w_in
    c1 = GLA_KEY_DIM
    c2 = 2 * GLA_KEY_DIM
    c3 = c2 + GLA_VAL_DIM
    c4 = c3 + GLA_VAL_DIM
    c5 = c4 + GATE_RANK
    q, k, v, gate, r_f, r_b = jnp.split(p, [c1, c2, c3, c4, c5], axis=-1)
    q = q.reshape(B, S, GLA_HEADS, GLA_HEAD_K) * (GLA_HEAD_K ** -0.5)
    k = k.reshape(B, S, GLA_HEADS, GLA_HEAD_K)
    v = v.reshape(B, S, GLA_HEADS, GLA_HEAD_V)
    log_a_f = jax.nn.log_sigmoid((r_f @ w_up_f + b_up_f).astype(jnp.float32)) / GATE_TAU
    log_a_b = jax.nn.log_sigmoid((r_b @ w_up_b + b_up_b).astype(jnp.float32)) / GATE_TAU
    log_a_f = log_a_f.reshape(B, S, GLA_HEADS, GLA_HEAD_K)
    log_a_b = log_a_b.reshape(B, S, GLA_HEADS, GLA_HEAD_K)
    o_fwd = gla_chunked(q, k, v, log_a_f)
    flip = lambda t: jnp.flip(t, axis=1)
    o_bwd = flip(gla_chunked(flip(q), flip(k), flip(v), flip(log_a_b)))
    o = o_fwd + o_bwd
    o = o * lax.rsqrt(jnp.mean(o * o, axis=-1, keepdims=True) + EPS) * g_norm.astype(jnp.float32)
    o = o.reshape(B, S, GLA_VAL_DIM) * jax.nn.silu(gate.astype(jnp.float32))
    return o.astype(h.dtype) @ w_out


def relu2_mlp(h, w1, w2):
    a = jax.nn.relu(h @ w1)
    return (a * a) @ w2


def setup_inputs(seed: int = 0) -> dict:
    key = jax.random.key(seed)
    ks = jax.random.split(key, 16)
    f32 = jnp.float32
    nrm = lambda k, shape, s: (jax.random.normal(k, shape, f32) * s).astype(f32)
    return {
        "x": nrm(ks[0], (BATCH, SEQ, D_MODEL), 1.0),
        "norm_mix": 1.0 + nrm(ks[1], (DEPTH, D_MODEL), 0.02),
        "norm_mlp": 1.0 + nrm(ks[2], (DEPTH, D_MODEL), 0.02),
        "norm_final": 1.0 + nrm(ks[3], (D_MODEL,), 0.02),
        "pool_w": nrm(ks[4], (N_POOL_LAYERS, N_POOL_GROUPS, POOL_GROUP, POOL_GROUP), POOL_GROUP ** -0.5),
        "pool_scale": 1.0 + nrm(ks[5], (N_POOL_LAYERS, D_MODEL), 0.02),
        "gla_w_in": nrm(ks[6], (N_GLA_LAYERS, D_MODEL, PROJ_WIDTH), D_MODEL ** -0.5),
        "gla_w_up_f": nrm(ks[7], (N_GLA_LAYERS, GATE_RANK, GLA_KEY_DIM), GATE_RANK ** -0.5),
        "gla_b_up_f": nrm(ks[8], (N_GLA_LAYERS, GLA_KEY_DIM), 0.1),
        "gla_w_up_b": nrm(ks[9], (N_GLA_LAYERS, GATE_RANK, GLA_KEY_DIM), GATE_RANK ** -0.5),
        "gla_b_up_b": nrm(ks[10], (N_GLA_LAYERS, GLA_KEY_DIM), 0.1),
        "gla_g_norm": 1.0 + nrm(ks[11], (N_GLA_LAYERS, GLA_HEAD_V), 0.02),
        "gla_w_out": nrm(ks[12], (N_GLA_LAYERS, GLA_VAL_DIM, D_MODEL), GLA_VAL_DIM ** -0.5),
        "mlp_w_in": nrm(ks[13], (DEPTH, D_MODEL, D_FF), D_MODEL ** -0.5),
        "mlp_w_out": nrm(ks[14], (DEPTH, D_FF, D_MODEL), D_FF ** -0.5),
    }


def reference(x, norm_mix, norm_mlp, norm_final, pool_w, pool_scale, gla_w_in, gla_w_up_f,
              gla_b_up_f, gla_w_up_b, gla_b_up_b, gla_g_norm, gla_w_out, mlp_w_in, mlp_w_out):
    h = x
    for layer in range(DEPTH):
        j = layer // N_MIXERS
        hn = rmsnorm(h, norm_mix[layer])
        if layer % N_MIXERS == 0:
            h = h + pool_mixer(hn, pool_w[j], pool_scale[j])
        else:
            h = h + gla_mixer(hn, gla_w_in[j], gla_w_up_f[j], gla_b_up_f[j], gla_w_up_b[j],
                              gla_b_up_b[j], gla_g_norm[j], gla_w_out[j])
        hn = rmsnorm(h, norm_mlp[layer])
        h = h + relu2_mlp(hn, mlp_w_in[layer], mlp_w_out[layer])
    return rmsnorm(h, norm_final)
```

```python
import numpy as np
from contextlib import ExitStack
import concourse.bass as bass
import concourse.mybir as mybir
from concourse.bass_utils import run_bass_kernel_spmd

F32 = mybir.dt.float32
BF16 = mybir.dt.bfloat16
AF = mybir.ActivationFunctionType
ALU = mybir.AluOpType

ENGS = ("pe", "act", "dve", "pool", "sp")

S = 2048
D = 4096
DFF = 16384
T = 512
NT = S // T
KC = D // 128
EPS = 1e-6
POOL_WINDOWS = (2, 4, 8, 16)
HK = 512
HV = 1024
NH = 4
PROJ = 12320


class Op:
    __slots__ = ("eng", "fn", "reads", "writes", "dma", "idx", "waits", "signal", "mile", "tok", "bar")

    def __init__(self, eng, fn, reads, writes, dma):
        self.eng = eng
        self.fn = fn
        self.reads = reads
        self.writes = writes
        self.dma = dma
        self.waits = []
        self.signal = False
        self.mile = 0
        self.tok = None
        self.bar = None


class Prog:
    def __init__(self, nc, dma_slots=None):
        self.nc = nc
        self.ops = []
        self.dma_slots = dma_slots or {"sp": 8, "pool": 4, "act": 2}

    def op(self, eng, fn, reads=(), writes=(), dma=False):
        o = Op(eng, fn, tuple(reads), tuple(writes), dma)
        o.idx = len(self.ops)
        self.ops.append(o)
        return o

    def pe(self, fn, r=(), w=()):
        return self.op("pe", fn, r, w)

    def act(self, fn, r=(), w=()):
        return self.op("act", fn, r, w)

    def dve(self, fn, r=(), w=()):
        return self.op("dve", fn, r, w)

    def pool(self, fn, r=(), w=()):
        return self.op("pool", fn, r, w)

    def dma(self, out, in_, r=(), w=(), q="sp", **kw):
        return self.op(q, lambda e: e.dma_start(out=out, in_=in_, **kw), r, w, dma=True)

    def barrier(self, scratch):
        bid = len(self.ops)
        marks = {}
        for e, s in scratch.items():
            if e == "pe":
                continue
            o = self.op(e, (lambda s_, e_: (lambda en: en.memzero(s_) if e_ == 'act' else en.memset(s_, 0.0)))(s, e), (), ())
            o.bar = ("mark", bid)
            marks[e] = o
        for e in ENGS:
            o = self.op(e, None, (), ())
            o.bar = ("join", bid)

    def finalize(self):
        ops = self.ops
        last_w = {}
        readers = {}
        known = {e: {} for e in ENGS}
        known_dma = {e: set() for e in ENGS}
        last_sig = {e: None for e in ENGS}
        out_dma = {e: [] for e in ENGS}
        for o in ops:
            if o.bar is not None and o.bar[0] == "join":
                for pe_ in ("pe", "act", "dve", "pool"):
                    d = last_sig[pe_]
                    if d is None or pe_ == o.eng:
                        continue
                    if known[o.eng].get(pe_, -1) >= d:
                        continue
                    known[o.eng][pe_] = d
                    o.waits.append(("eng", pe_, d))
                    ops[d].signal = True
                for q in ENGS:
                    k = self.dma_slots.get(q, 0)
                    for d in out_dma[q][-k:] if k else []:
                        if d not in known_dma[o.eng]:
                            known_dma[o.eng].add(d)
                            o.waits.append(("dma", d))
                continue
            raw = set()
            oth = set()
            for r in o.reads:
                lw = last_w.get(r)
                if lw is not None:
                    raw.add(lw)
            for w in o.writes:
                lw = last_w.get(w)
                if lw is not None:
                    oth.add(lw)
                for rd in readers.get(w, ()):
                    oth.add(rd)
            for r in o.reads:
                readers.setdefault(r, []).append(o.idx)
            for w in o.writes:
                last_w[w] = o.idx
                readers[w] = []
            need_eng = {}
            for d in raw | oth:
                if d == o.idx:
                    continue
                p = ops[d]
                if p.dma:
                    if d not in known_dma[o.eng]:
                        o.waits.append(("dma", d))
                        known_dma[o.eng].add(d)
                    continue
                if p.eng == o.eng:
                    if o.eng == "pe" or d not in raw:
                        continue
                need_eng[p.eng] = max(need_eng.get(p.eng, -1), d)
            for pe_, d in need_eng.items():
                if known[o.eng].get(pe_, -1) >= d:
                    continue
                known[o.eng][pe_] = d
                o.waits.append(("eng", pe_, d))
                ops[d].signal = True
            if o.dma:
                out_dma[o.eng].append(o.idx)
            elif o.eng != "sp":
                last_sig[o.eng] = o.idx
        cnt = {e: 0 for e in ENGS}
        dcnt = {e: 0 for e in ENGS}
        for o in ops:
            if o.dma:
                k = self.dma_slots[o.eng]
                n = dcnt[o.eng]
                o.tok = (n % k, 16 * (n // k + 1), n)
                dcnt[o.eng] += 1
            elif o.signal:
                cnt[o.eng] += 1
                o.mile = cnt[o.eng]
        self.n_signal = cnt
        self.n_dma = dcnt

    def emit(self):
        nc = self.nc
        self.finalize()
        ops = self.ops
        with ExitStack() as es:
            esem = {e: es.enter_context(nc.semaphore("s_" + e)) for e in ENGS}
            dsem = {
                q: [es.enter_context(nc.semaphore("d_%s%d" % (q, i))) for i in range(k)]
                for q, k in self.dma_slots.items()
            }
            block = es.enter_context(nc.Block())
            per = {e: [o for o in ops if o.eng == e] for e in ENGS}

            def run(engname, eng):
                k = self.dma_slots.get(engname, 0)
                my_dmas = [o for o in per[engname] if o.dma]
                for o in per[engname]:
                    for wt in o.waits:
                        if wt[0] == "eng":
                            p = ops[wt[2]]
                            eng.wait_ge(esem[p.eng], p.mile)
                        else:
                            p = ops[wt[1]]
                            eng.wait_ge(dsem[p.eng][p.tok[0]], p.tok[1])
                    if o.fn is None:
                        continue
                    if o.dma:
                        slot, val, n = o.tok
                        if n >= k:
                            eng.wait_ge(dsem[engname][slot], val - 16)
                        o.fn(eng).then_inc(dsem[engname][slot], 16)
                    else:
                        ins = o.fn(eng)
                        if o.signal:
                            ins.then_inc(esem[engname], 1)
                if my_dmas:
                    last = {}
                    for o in my_dmas:
                        last[o.tok[0]] = o.tok[1]
                    for slot, val in last.items():
                        eng.wait_ge(dsem[engname][slot], val)

            @block.sync
            def _(e):
                run("sp", e)

            @block.tensor
            def _(e):
                run("pe", e)

            @block.scalar
            def _(e):
                run("act", e)

            @block.vector
            def _(e):
                run("dve", e)

            @block.gpsimd
            def _(e):
                run("pool", e)


def _pool_A(i, j, w):
    lo = max(i - w // 2, 0)
    hi = min(i + w // 2, S)
    return (1.0 / (hi - lo)) if lo <= j < hi else 0.0


def make_consts():
    bands = np.zeros((4, 5, 128, 128), np.float32)
    for g, w in enumerate(POOL_WINDOWS):
        def blk(I, J):
            m = np.zeros((128, 128), np.float32)
            for ii in range(128):
                i = I * 128 + ii
                for j in range(max(i - 16, J * 128), min(i + 17, (J + 1) * 128)):
                    if j < 0 or j >= S:
                        continue
                    m[j - J * 128, ii] = _pool_A(i, j, w) - (1.0 if i == j else 0.0)
            return m
        bands[g, 0] = blk(5, 5)
        bands[g, 1] = blk(5, 4)
        bands[g, 2] = blk(5, 6)
        bands[g, 3] = blk(0, 0)
        bands[g, 4] = blk(15, 15)
    ident = np.eye(128, dtype=np.float32)
    jj = np.arange(128)[:, None]
    ii = np.arange(128)[None, :]
    tri = np.stack([(ii >= jj), (ii <= jj)]).astype(np.float32)
    return {"c_bands": bands, "c_ident": ident, "c_tri": tri}


class Builder:
    def __init__(self, phases=(1, 2, 3, 4), tiles=None, ext=()):
        self.phases = phases
        self.tiles = list(range(NT)) if tiles is None else list(tiles)
        self.ext = set(ext)
        self.nc = bass.Bass("TRN2", target_bir_lowering=False)
        self.P = Prog(self.nc)
        self.in_names = []

    def din(self, name, shape, dt=F32, ph=(1, 2, 3, 4)):
        if not any(p in self.phases for p in ph):
            return None
        self.in_names.append(name)
        return self.nc.dram_tensor(name, list(shape), dt, kind="ExternalInput").ap()

    def dscr(self, name, shape, dt, produced_by):
        if name in self.ext:
            kind = "ExternalOutput" if produced_by in self.phases else "ExternalInput"
        else:
            kind = "Internal"
        return self.nc.dram_tensor(name, list(shape), dt, kind=kind).ap()

    def reset_arena(self):
        self.off = self.persist_off

    def alloc(self, nfree, dt=F32, parts=128, shape=None):
        nbytes = nfree * (4 if dt == F32 else 2)
        nw = (nbytes + 3) // 4
        nw = (nw + 7) // 8 * 8
        assert self.off + nw <= self.arena_words, ("SBUF arena overflow", self.off, nw, self.arena_words)
        v = self.arena[0:parts, self.off:self.off + nw]
        self.off += nw
        if dt != F32:
            v = v.bitcast(dt)
        v = v[:, 0:nfree]
        if shape is not None:
            assert len(shape) == 2
            v = v.rearrange("p (a b) -> p a b", a=shape[0], b=shape[1])
        return v

    def build(self):
        nc = self.nc
        P = self.P
        with ExitStack() as es:
            self.arena_words = 53200
            self.arena = es.enter_context(nc.sbuf_tensor("arena", [128, self.arena_words], F32))
            self.ps = [es.enter_context(nc.psum_tensor("ps%d" % i, [128, 512], F32)) for i in range(8)]
            self.off = 0
            self.declare_dram()
            self.setup_consts()
            self.persist_off = self.off
            first = True
            for ph in (1, 2, 3, 4):
                if ph not in self.phases:
                    continue
                if not first:
                    P.barrier(self.bar_scratch)
                first = False
                self.reset_arena()
                getattr(self, "phase%d" % ph)()
            P.emit()
        return nc

    def declare_dram(self):
        self.x = self.din("x", [S, D], ph=(1,))
        self.norm_mix = self.din("norm_mix", [2, D])
        self.norm_mlp = self.din("norm_mlp", [2, D])
        self.norm_final = self.din("norm_final", [1, D], ph=(4,))
        self.pool_w = self.din("pool_w", [4, 1024, 1024], ph=(1,))
        self.pool_scale = self.din("pool_scale", [1, D], ph=(1,))
        self.w_in = self.din("gla_w_in", [D, PROJ], ph=(2,))
        self.w_up = [self.din("gla_w_up_f", [16, 2048], ph=(2,)), self.din("gla_w_up_b", [16, 2048], ph=(2,))]
        self.b_up = [self.din("gla_b_up_f", [1, 2048], ph=(2,)), self.din("gla_b_up_b", [1, 2048], ph=(2,))]
        self.g_norm = self.din("gla_g_norm", [1, HV], ph=(4,))
        self.w_out = self.din("gla_w_out", [D, D], ph=(4,))
        self.mlp_w1 = self.din("mlp_w_in", [2, D, DFF], ph=(1, 3, 4))
        self.mlp_w2 = self.din("mlp_w_out", [2, DFF, D], ph=(1, 3, 4))
        self.c_bands = self.din("c_bands", [4, 5, 128, 128], ph=(1,))
        self.c_ident = self.din("c_ident", [128, 128])
        self.c_tri = self.din("c_tri", [2, 128, 128], ph=(3,))
        self.h1 = self.dscr("h1", [S, D], F32, 1)
        self.qT = self.dscr("qT", [2048, S], F32, 2)
        self.kT = self.dscr("kT", [2048, S], F32, 2)
        self.gT = self.dscr("gT", [2, 2048, S], F32, 2)
        self.v = self.dscr("v", [S, D], BF16, 2)
        self.gate = self.dscr("gate", [S, D], F32, 2)
        self.ofb = self.dscr("ofb", [2, S, D], F32, 3)
        self.w1s = self.nc.dram_tensor("w1s", [2, 64, 128, KC * 256], BF16, kind="Internal").ap()
        self.w2s = self.nc.dram_tensor("w2s", [2, 128, 128, 8 * 512], BF16, kind="Internal").ap()

        self.out = self.nc.dram_tensor("out", [S, D], F32, kind="ExternalOutput").ap()

    def setup_consts(self):
        P = self.P
        self.identf = self.alloc(128)
        self.identb = self.alloc(128, BF16)
        self.gcol = self.alloc(4 * KC, shape=(4, KC))
        self.bscr = {e: self.alloc(8) for e in ("act", "dve", "pool")}
        self.bar_scratch = {e: v[:, 0:1] for e, v in self.bscr.items()}
        P.dma(self.identf, self.c_ident, w=["identf"])
        P.dve(lambda e: e.tensor_copy(out=self.identb, in_=self.identf), r=["identf"], w=["identb"])
        srcs = [self.norm_mlp[0:1, :], self.norm_mix[1:2, :], self.norm_mlp[1:2, :]]
        for n, src in enumerate(srcs):
            P.dma(self.gcol[:, n, :], src.rearrange("o (kc p) -> p (o kc)", p=128), w=["gcol"],
                  allow_slow_non_contiguous=True)

    def rstd_of(self, src, rstd_ap, junk, keys_r, n_elems, key_w, junk_key="xnb"):
        P = self.P
        P.pool(lambda e: e.memset(rstd_ap, 0.0), w=[key_w])
        P.act(lambda e: e.activation(out=junk, in_=src, func=AF.Square, accum_out=rstd_ap),
              r=keys_r, w=[key_w, junk_key])
        P.act(lambda e: e.activation(out=rstd_ap, in_=rstd_ap, func=AF.Sqrt, bias=EPS, scale=1.0 / n_elems),
              r=[key_w], w=[key_w])
        P.dve(lambda e: e.reciprocal(out=rstd_ap, in_=rstd_ap), r=[key_w], w=[key_w])

    def norm_transpose(self, hsrc, hkeys, tb, gidx, xnb, rstd_ap, rkey):
        P = self.P
        self.rstd_of(hsrc, rstd_ap, xnb, hkeys, D, rkey)
        P.dve(lambda e: e.tensor_scalar_mul(out=xnb, in0=hsrc, scalar1=rstd_ap), r=list(hkeys) + [rkey], w=["xnb"])
        self.transpose_into_hnT(xnb, "xnb", tb, gidx)

    def transpose_into_hnT(self, src_bf, skey, tb, gidx, kc0=0, nkc=KC):
        P = self.P
        for g8 in range(0, nkc, 8):
            bank = self.tr_banks[self.tr_i % len(self.tr_banks)]
            self.tr_i += 1
            pt = self.ps[bank][:].bitcast(BF16)
            for j in range(8):
                kc = g8 + j
                P.pe((lambda kc_, j_, pt_: (lambda e: e.transpose(pt_[:, j_ * 128:(j_ + 1) * 128],
                                                                     src_bf[:, kc_ * 128:(kc_ + 1) * 128], self.identb)))(kc, j, pt),
                     r=[skey, "identb"], w=[("ps", bank)])
            dst = self.hnT[:, kc0 + g8:kc0 + g8 + 8, tb * 128:(tb + 1) * 128]
            srcv = pt.rearrange("p (a b) -> p a b", a=8, b=128)
            if gidx is None:
                P.act((lambda d_, s_: (lambda e: e.copy(out=d_, in_=s_)))(dst, srcv), r=[("ps", bank)], w=[("hnT", tb)])
            else:
                gc = self.gcol[:, gidx, kc0 + g8:kc0 + g8 + 8]
                P.dve((lambda d_, s_, g_: (lambda e: e.tensor_tensor(out=d_, in0=s_, in1=g_.unsqueeze(2).to_broadcast([128, 8, 128]), op=ALU.mult)))(dst, srcv, gc),
                      r=[("ps", bank), "gcol"], w=[("hnT", tb)])

    def next_mm_bank(self):
        b = self.mm_banks[self.mm_i % len(self.mm_banks)]
        self.mm_i += 1
        return b

    def alloc_mlp(self):
        self.WA = [self.alloc(KC * 256, BF16, shape=(KC, 256)) for _ in range(2)]
        self.WB = [self.alloc(8 * 512, BF16, shape=(8, 512)) for _ in range(2)]
        self.aT = [self.alloc(8 * 512, BF16, shape=(8, 512)) for _ in range(2)]
        self.rtmp = [self.alloc(512) for _ in range(2)]
        self.WB = self.WB + [self.xnb.rearrange("p (a b) -> p a b", a=8, b=512)]
        self.WBk = [("WB", 0), ("WB", 1), "xnb"]
        self.wa_i = 0
        self.wb_i = 0
        self.rt_i = 0

    def w1_src(self, layer, piece):
        w1 = self.mlp_w1[layer].rearrange("(kc p) f -> p kc f", p=128)
        return w1[:, :, piece * 256:(piece + 1) * 256]

    def w2_src(self, layer, piece):
        w2 = self.mlp_w2[layer].rearrange("(fc p) n -> p fc n", p=128)
        bb, ob = piece // 8, piece % 8
        return w2[:, bb * 8:(bb + 1) * 8, ob * 512:(ob + 1) * 512]

    def w1_scr(self, layer, piece):
        return self.w1s[layer, piece].rearrange("p (kc j) -> p kc j", kc=KC)

    def w2_scr(self, layer, piece):
        return self.w2s[layer, piece].rearrange("p (f j) -> p f j", f=8)

    def convert_mlp_weights(self, layer):
        P = self.P
        for b in range(DFF // 1024):
            for qtr in range(4):
                pc = b * 4 + qtr
                P.dma(self.w1_scr(layer, pc), self.w1_src(layer, pc), w=[("w1s", layer, pc)], q="pool")
            for ob in range(8):
                pc = b * 8 + ob
                P.dma(self.w2_scr(layer, pc), self.w2_src(layer, pc), w=[("w2s", layer, pc)], q="pool")

    def mlp(self, layer, tidx, ntiles):
        P = self.P
        WA, WB, aT, hnT = self.WA, self.WB, self.aT, self.hnT
        NB = DFF // 1024
        hn_keys = [("hnT", tb) for tb in range(4)]

        def load(buf, bkey, scr, src, skey, kind, pc):
            if kind == 1:
                store_at = 0
            else:
                store_at = 1 if pc % 2 == 0 else 2
            store_at = min(store_at, ntiles - 1)
            if tidx > store_at:
                P.dma(buf, scr, r=[skey], w=[bkey], q="sp")
            else:
                P.dma(buf, src, w=[bkey], q="pool")
                if tidx == store_at and ntiles > 1:
                    P.dma(scr, buf, r=[bkey], w=[skey], q="sp")

        for b in range(NB + 1):
            for qtr in range(4):
                if b < NB:
                    wi = self.wa_i % 2
                    self.wa_i += 1
                    pc = b * 4 + qtr
                    load(WA[wi], ("WA", wi), self.w1_scr(layer, pc), self.w1_src(layer, pc), ("w1s", layer, pc), 1, pc)
                    for cc in range(2):
                        fl = qtr * 2 + cc
                        bank = self.next_mm_bank()
                        pm = self.ps[bank]
                        for kc in range(KC):
                            P.pe((lambda kc_, cc_, wi_, pm_: (lambda e: e.matmul(pm_[:], lhsT=WA[wi_][:, kc_, cc_ * 128:(cc_ + 1) * 128],
                                                                                  rhs=hnT[:, kc_, :], start=(kc_ == 0), stop=(kc_ == KC - 1))))(kc, cc, wi, pm),
                                 r=[("WA", wi)] + hn_keys, w=[("ps", bank)])
                        ri = self.rt_i % 2
                        self.rt_i += 1
                        rt = self.rtmp[ri]
                        P.act((lambda rt_, pm_: (lambda e: e.activation(out=rt_, in_=pm_[:], func=AF.Relu)))(rt, pm),
                              r=[("ps", bank)], w=[("rtmp", ri)])
                        P.dve((lambda rt_, b_, fl_: (lambda e: e.tensor_tensor(out=aT[b_ % 2][:, fl_, :], in0=rt_, in1=rt_, op=ALU.mult)))(rt, b, fl),
                              r=[("rtmp", ri)], w=[("aT", b % 2, fl)])
                if b > 0:
                    bb = b - 1
                    for ob in (2 * qtr, 2 * qtr + 1):
                        wi = self.wb_i % 3
                        self.wb_i += 1
                        pc = bb * 8 + ob
                        load(WB[wi], self.WBk[wi], self.w2_scr(layer, pc), self.w2_src(layer, pc), ("w2s", layer, pc), 2, pc)
                        for tb in range(4):
                            bank = self.next_mm_bank()
                            pm = self.ps[bank]
                            for f in range(8):
                                P.pe((lambda f_, tb_, wi_, pm_, bb_: (lambda e: e.matmul(pm_[:], lhsT=aT[bb_ % 2][:, f_, tb_ * 128:(tb_ + 1) * 128],
                                                                                         rhs=WB[wi_][:, f_, :], start=(f_ == 0), stop=(f_ == 7))))(f, tb, wi, pm, bb),
                                     r=[self.WBk[wi], ("aT", bb % 2, f)], w=[("ps", bank)])
                            hs = self.h[:, tb, ob * 512:(ob + 1) * 512]
                            P.dve((lambda hs_, pm_: (lambda e: e.tensor_tensor(out=hs_, in0=hs_, in1=pm_[:], op=ALU.add)))(hs, pm),
                                  r=[("ps", bank), ("h", tb)], w=[("h", tb)])

    def phase1(self):
        P = self.P
        self.h = self.alloc(4 * D, shape=(4, D))
        self.hnT = self.alloc(KC * 512, BF16, shape=(KC, 512))
        self.xnb = self.alloc(D, BF16)
        self.alloc_mlp()
        h_, WB_ = self.h, self.WB
        bands = self.alloc(20 * 128, BF16, shape=(20, 128))
        bcB = self.alloc(1024)
        bcC = self.alloc(512)
        xnt = self.alloc(512)
        xhi = self.alloc(4 * 512, BF16, shape=(4, 512))
        xlo = self.alloc(4 * 512, BF16, shape=(4, 512))
        hhi = self.alloc(2 * 512, BF16, shape=(2, 512))
        hlo = self.alloc(2 * 512, BF16, shape=(2, 512))
        rstd = self.alloc(8)
        rstd_h = self.alloc(8)
        ssq_h = self.alloc(16, shape=(2, 8))
        rkeep = self.alloc(8)
        prev_ti = None
        self.tr_banks = [4, 5]
        self.mm_banks = [0, 1, 2, 3]
        dps_banks = [6, 7]
        self.tr_i = 0
        self.mm_i = 0
        dps_i = 0
        P.dma(bands, self.c_bands.rearrange("g k j i -> j (g k) i"), w=["bands"], q="pool")
        dT = self.aT[0]
        tmpA = self.rtmp[0]
        tmpB = self.rtmp[1]
        for ti in self.tiles:
            t0 = ti * T
            for tb in range(4):
                P.dma(self.h[:, tb, :], self.x[t0 + tb * 128:t0 + (tb + 1) * 128, :], w=[("h", tb)])
            for tb in range(4):
                self.rstd_of(self.h[:, tb, :], rstd[:, tb:tb + 1], self.xnb, [("h", tb)], D, ("rstd", tb))
            hp = []
            if ti > 0:
                hp.append((0, t0 - 128))
            if ti < NT - 1:
                hp.append((1, t0 + T))
            xf = self.xnb.bitcast(F32)
            for (sl, r0) in hp:
                if sl == 0 and prev_ti == ti - 1:
                    P.dve((lambda c_: (lambda e: e.tensor_copy(out=rstd_h[:, 0:1], in_=rkeep[:, c_:c_ + 1])))((ti - 1) % 2),
                          r=[("rkeep", (ti - 1) % 2)], w=["rstd_h"])
                    continue
                P.pool((lambda sl_: (lambda e: e.memset(ssq_h[:, sl_, :], 0.0)))(sl), w=["ssq_h"])
                for pc in range(2):
                    P.dma(xf, self.x[r0:r0 + 128, pc * 2048:(pc + 1) * 2048], w=["xnb"])
                    P.act((lambda sl_, pc_: (lambda e: e.activation(out=xf, in_=xf, func=AF.Square, accum_out=ssq_h[:, sl_, pc_:pc_ + 1])))(sl, pc),
                          r=["xnb"], w=["xnb", "ssq_h"])
                P.dve((lambda sl_: (lambda e: e.tensor_reduce(out=rstd_h[:, sl_:sl_ + 1], in_=ssq_h[:, sl_, 0:2], axis=mybir.AxisListType.X, op=ALU.add)))(sl),
                      r=["ssq_h"], w=["rstd_h"])
                P.act((lambda sl_: (lambda e: e.activation(out=rstd_h[:, sl_:sl_ + 1], in_=rstd_h[:, sl_:sl_ + 1], func=AF.Sqrt, bias=EPS, scale=1.0 / D)))(sl),
                      r=["rstd_h"], w=["rstd_h"])
                P.dve((lambda sl_: (lambda e: e.reciprocal(out=rstd_h[:, sl_:sl_ + 1], in_=rstd_h[:, sl_:sl_ + 1])))(sl), r=["rstd_h"], w=["rstd_h"])
            P.dve((lambda c_: (lambda e: e.tensor_copy(out=rkeep[:, c_:c_ + 1], in_=rstd[:, 3:4])))(ti % 2), r=[("rstd", 3)], w=[("rkeep", ti % 2)])
            prev_ti = ti
            hslots = dict(hp)
            for g in range(4):
                P.dma(bcB, self.pool_scale[0:1, g * 1024:(g + 1) * 1024].partition_broadcast(128), w=["bcB"])
                for hh in range(2):
                    c0 = g * 1024 + hh * 512
                    P.dma(bcC, self.norm_mix[0:1, c0:c0 + 512].partition_broadcast(128), w=["bcC"])
                    for tb in range(4):
                        P.dve((lambda tb_, c0_: (lambda e: e.scalar_tensor_tensor(out=xnt, in0=h_[:, tb_, c0_:c0_ + 512], scalar=rstd[:, tb_:tb_ + 1],
                                                                                   in1=bcC, op0=ALU.mult, op1=ALU.mult)))(tb, c0),
                              r=[("h", tb), ("rstd", tb), "bcC"], w=["xnt"])
                        P.dve((lambda tb_: (lambda e: e.tensor_copy(out=xhi[:, tb_, :], in_=xnt)))(tb), r=["xnt"], w=[("xhi", tb)])
                        P.dve((lambda tb_: (lambda e: e.tensor_tensor(out=xlo[:, tb_, :], in0=xnt, in1=xhi[:, tb_, :], op=ALU.subtract)))(tb),
                              r=["xnt", ("xhi", tb)], w=[("xlo", tb)])
                    for (sl, r0) in hp:
                        P.dma(tmpA, self.x[r0:r0 + 128, c0:c0 + 512], w=[("rtmp", 0)])
                        P.dve((lambda sl_: (lambda e: e.scalar_tensor_tensor(out=tmpA, in0=tmpA, scalar=rstd_h[:, sl_:sl_ + 1], in1=bcC, op0=ALU.mult, op1=ALU.mult)))(sl),
                              r=[("rtmp", 0), "rstd_h", "bcC"], w=[("rtmp", 0)])
                        P.dve((lambda sl_: (lambda e: e.tensor_copy(out=hhi[:, sl_, :], in_=tmpA)))(sl), r=[("rtmp", 0)], w=[("hhi", sl)])
                        P.dve((lambda sl_: (lambda e: e.tensor_tensor(out=hlo[:, sl_, :], in0=tmpA, in1=hhi[:, sl_, :], op=ALU.subtract)))(sl),
                              r=[("rtmp", 0), ("hhi", sl)], w=[("hlo", sl)])
                    for kc in range(4):
                        bank = dps_banks[dps_i % 2]
                        dps_i += 1
                        pm = self.ps[bank]
                        ks = slice(kc * 128, (kc + 1) * 128)
                        for tb in range(4):
                            gb = ti * 4 + tb
                            terms = []
                            bself = 3 if gb == 0 else (4 if gb == S // 128 - 1 else 0)
                            bs_ = bands[:, g * 5 + bself, :]
                            terms.append((xhi[:, tb, ks], bs_, [("xhi", tb)]))
                            terms.append((xlo[:, tb, ks], bs_, [("xlo", tb)]))
                            bp_ = bands[:, g * 5 + 1, :]
                            if tb > 0:
                                terms.append((xhi[:, tb - 1, ks], bp_, [("xhi", tb - 1)]))
                                terms.append((xlo[:, tb - 1, ks], bp_, [("xlo", tb - 1)]))
                            elif 0 in hslots:
                                terms.append((hhi[:, 0, ks], bp_, [("hhi", 0)]))
                                terms.append((hlo[:, 0, ks], bp_, [("hlo", 0)]))
                            bn_ = bands[:, g * 5 + 2, :]
                            if tb < 3:
                                terms.append((xhi[:, tb + 1, ks], bn_, [("xhi", tb + 1)]))
                                terms.append((xlo[:, tb + 1, ks], bn_, [("xlo", tb + 1)]))
                            elif 1 in hslots:
                                terms.append((hhi[:, 1, ks], bn_, [("hhi", 1)]))
                                terms.append((hlo[:, 1, ks], bn_, [("hlo", 1)]))
                            for n, (l_, r_, k_) in enumerate(terms):
                                P.pe((lambda l__, r__, n_, nt_, tb_, pm_: (lambda e: e.matmul(pm_[:, tb_ * 128:(tb_ + 1) * 128], lhsT=l__, rhs=r__,
                                                                                              start=(n_ == 0), stop=(n_ == nt_ - 1))))(l_, r_, n, len(terms), tb, pm),
                                     r=k_ + ["bands"], w=[("ps", bank)])
                        P.act((lambda pm_, j_: (lambda e: e.copy(out=dT[:, j_, :], in_=pm_[:])))(pm, hh * 4 + kc),
                              r=[("ps", bank)], w=[("aT", 0, hh * 4 + kc)])
                wp = self.pool_w[g].rearrange("(kc p) n -> p kc n", p=128)
                for cb in range(2):
                    wi = self.wb_i % 2
                    self.wb_i += 1
                    P.dma(self.WB[wi], wp[:, :, cb * 512:(cb + 1) * 512], w=[("WB", wi)], q="pool")
                    for tb in range(4):
                        bank = self.next_mm_bank()
                        pm = self.ps[bank]
                        for kc in range(8):
                            P.pe((lambda kc_, tb_, wi_, pm_: (lambda e: e.matmul(pm_[:], lhsT=dT[:, kc_, tb_ * 128:(tb_ + 1) * 128], rhs=WB_[wi_][:, kc_, :],
                                                                                  start=(kc_ == 0), stop=(kc_ == 7))))(kc, tb, wi, pm),
                                 r=[("WB", wi), ("aT", 0, kc)], w=[("ps", bank)])
                        hs = self.h[:, tb, g * 1024 + cb * 512:g * 1024 + (cb + 1) * 512]
                        P.dve((lambda pm_, cb_: (lambda e: e.tensor_tensor(out=tmpA, in0=pm_[:], in1=bcB[:, cb_ * 512:(cb_ + 1) * 512], op=ALU.mult)))(pm, cb),
                              r=[("ps", bank), "bcB"], w=[("rtmp", 0)])
                        P.dve((lambda hs_: (lambda e: e.tensor_tensor(out=hs_, in0=hs_, in1=tmpA, op=ALU.add)))(hs),
                              r=[("rtmp", 0), ("h", tb)], w=[("h", tb)])
            for tb in range(4):
                self.norm_transpose(self.h[:, tb, :], [("h", tb)], tb, 0, self.xnb, rstd[:, tb:tb + 1], ("rstd", tb))
            self.mlp(0, self.tiles.index(ti), len(self.tiles))
            for tb in range(4):
                P.dma(self.h1[t0 + tb * 128:t0 + (tb + 1) * 128, :], self.h[:, tb, :], r=[("h", tb)], w=[("h1", ti, tb)])


    def phase2(self):
        P = self.P
        self.hnT = self.alloc(KC * 1024, BF16, shape=(KC, 1024))
        hnT = self.hnT
        self.xnb = self.alloc(D, BF16)
        WA = [self.alloc(KC * 256, BF16, shape=(KC, 256)) for _ in range(2)]
        wv_f = [self.alloc(KC * 256) for _ in range(2)]
        WV = [w.bitcast(BF16).rearrange("p (kc j) -> p kc j", kc=KC) for w in wv_f]
        hs = [w[:, 0:D] for w in wv_f]
        wr = self.alloc(KC * 128, BF16, shape=(KC, 128))
        wup = self.alloc(2 * 2048, BF16, shape=(2, 2048))
        negb = self.alloc(32, shape=(2, 16))
        rT = [self.alloc(512, BF16) for _ in range(2)]
        st_q = [self.alloc(512) for _ in range(2)]
        st_g = [self.alloc(512) for _ in range(2)]
        st_v = [self.alloc(512, BF16) for _ in range(2)]
        st_gate = [self.alloc(512) for _ in range(2)]
        rstd = self.alloc(8)
        self.tr_banks = [4, 5]
        self.mm_banks = [0, 1, 2, 3]
        self.tr_i = 0
        self.mm_i = 0
        win = self.w_in.rearrange("(kc p) n -> p kc n", p=128)
        P.dve(lambda e: e.memset(wup, 0.0), w=["wup"])
        P.dma(wup[96:112, 0, :], self.w_up[0], w=["wup"], q="pool")
        P.dma(wup[112:128, 1, :], self.w_up[1], w=["wup"], q="pool")
        for d in range(2):
            P.dma(negb[:, d, :], self.b_up[d].rearrange("o (c p) -> p (o c)", p=128), w=["negb"], allow_slow_non_contiguous=True)
        P.dve(lambda e: e.tensor_scalar_mul(out=negb, in0=negb, scalar1=-1.0), r=["negb"], w=["negb"])
        P.dma(wr, win[:, :, PROJ - 128:PROJ], w=["wr"], q="pool")
        qscale = float(HK ** -0.5)
        wa_i = 0
        wv_i = 0
        sq_i = 0
        sg_i = 0
        sv_i = 0
        pairs = [self.tiles[i:i + 2] for i in range(0, len(self.tiles), 2)]
        hn_keys = [("hnT", t8) for t8 in range(8)]
        for pair in pairs:
            for j, ti in enumerate(pair):
                t0 = ti * T
                for tb in range(4):
                    t8 = j * 4 + tb
                    hb = hs[t8 % 2]
                    P.dma(hb, self.h1[t0 + tb * 128:t0 + (tb + 1) * 128, :], r=[("h1", ti, tb)], w=[("WV", t8 % 2)])
                    self.norm_transpose(hb, [("WV", t8 % 2)], t8, 1, self.xnb, rstd[:, t8:t8 + 1], ("rstd", t8))
            for cp in range(16):
                wi = wa_i % 2
                wa_i += 1
                P.dma(WA[wi], win[:, :, cp * 256:(cp + 1) * 256], w=[("WA", wi)], q="pool")
                for j, ti in enumerate(pair):
                    t0 = ti * T
                    for cc in range(2):
                        c = cp * 2 + cc
                        bank = self.next_mm_bank()
                        pm = self.ps[bank]
                        for kc in range(KC):
                            P.pe((lambda kc_, cc_, wi_, pm_, j_: (lambda e: e.matmul(pm_[:], lhsT=WA[wi_][:, kc_, cc_ * 128:(cc_ + 1) * 128], rhs=hnT[:, kc_, j_ * 512:(j_ + 1) * 512],
                                                                                      start=(kc_ == 0), stop=(kc_ == KC - 1))))(kc, cc, wi, pm, j),
                                 r=[("WA", wi)] + hn_keys, w=[("ps", bank)])
                        si = sq_i % 2
                        sq_i += 1
                        if c < 16:
                            P.act((lambda si_, pm_: (lambda e: e.mul(out=st_q[si_], in_=pm_[:], mul=qscale)))(si, pm), r=[("ps", bank)], w=[("st_q", si)])
                            P.dma(self.qT[c * 128:(c + 1) * 128, t0:t0 + T], st_q[si], r=[("st_q", si)], w=[("qT", c, ti)])
                        else:
                            P.act((lambda si_, pm_: (lambda e: e.copy(out=st_q[si_], in_=pm_[:])))(si, pm), r=[("ps", bank)], w=[("st_q", si)])
                            P.dma(self.kT[(c - 16) * 128:(c - 15) * 128, t0:t0 + T], st_q[si], r=[("st_q", si)], w=[("kT", c - 16, ti)])
            for j, ti in enumerate(pair):
                bank = self.next_mm_bank()
                pm = self.ps[bank]
                for kc in range(KC):
                    P.pe((lambda kc_, pm_, j_: (lambda e: e.matmul(pm_[:], lhsT=wr[:, kc_, :], rhs=hnT[:, kc_, j_ * 512:(j_ + 1) * 512], start=(kc_ == 0), stop=(kc_ == KC - 1))))(kc, pm, j),
                         r=["wr"] + hn_keys, w=[("ps", bank)])
                P.act((lambda pm_, j_: (lambda e: e.copy(out=rT[j_], in_=pm_[:])))(pm, j), r=[("ps", bank)], w=[("rT", j)])
            for j, ti in enumerate(pair):
                t0 = ti * T
                for d in range(2):
                    for c in range(16):
                        bank = self.next_mm_bank()
                        pm = self.ps[bank]
                        P.pe((lambda d_, c_, pm_, j_: (lambda e: e.matmul(pm_[:], lhsT=wup[:, d_, c_ * 128:(c_ + 1) * 128], rhs=rT[j_], start=True, stop=True)))(d, c, pm, j),
                             r=["wup", ("rT", j)], w=[("ps", bank)])
                        si = sg_i % 2
                        sg_i += 1
                        P.act((lambda d_, c_, si_, pm_: (lambda e: e.activation(out=st_g[si_], in_=pm_[:], func=AF.Exp, bias=negb[:, d_, c_:c_ + 1], scale=-1.0)))(d, c, si, pm),
                              r=[("ps", bank), "negb"], w=[("st_g", si)])
                        P.act((lambda si_: (lambda e: e.activation(out=st_g[si_], in_=st_g[si_], func=AF.Ln, bias=1.0, scale=1.0)))(si),
                              r=[("st_g", si)], w=[("st_g", si)])
                        P.dve((lambda si_: (lambda e: e.tensor_scalar_mul(out=st_g[si_], in0=st_g[si_], scalar1=-1.0 / 16.0)))(si),
                              r=[("st_g", si)], w=[("st_g", si)])
                        P.dma(self.gT[d, c * 128:(c + 1) * 128, t0:t0 + T], st_g[si], r=[("st_g", si)], w=[("gT", d, c, ti)])
            for cb in range(16):
                wi = wv_i % 2
                wv_i += 1
                f0 = 4096 + cb * 512
                P.dma(WV[wi], win[:, :, f0:f0 + 512], w=[("WV", wi)], q="pool")
                for j, ti in enumerate(pair):
                    t0 = ti * T
                    for tb in range(4):
                        t8 = j * 4 + tb
                        bank = self.next_mm_bank()
                        pm = self.ps[bank]
                        for kc in range(KC):
                            P.pe((lambda kc_, t8_, wi_, pm_: (lambda e: e.matmul(pm_[:], lhsT=hnT[:, kc_, t8_ * 128:(t8_ + 1) * 128], rhs=WV[wi_][:, kc_, :],
                                                                                  start=(kc_ == 0), stop=(kc_ == KC - 1))))(kc, t8, wi, pm),
                                 r=[("WV", wi), ("hnT", t8)], w=[("ps", bank)])
                        si = sv_i % 2
                        sv_i += 1
                        rows = slice(t0 + tb * 128, t0 + (tb + 1) * 128)
                        if cb < 8:
                            P.act((lambda si_, pm_: (lambda e: e.copy(out=st_v[si_], in_=pm_[:])))(si, pm), r=[("ps", bank)], w=[("st_v", si)])
                            P.dma(self.v[rows, cb * 512:(cb + 1) * 512], st_v[si], r=[("st_v", si)], w=[("v", ti, tb, cb)])
                        else:
                            P.act((lambda si_, pm_: (lambda e: e.copy(out=st_gate[si_], in_=pm_[:])))(si, pm), r=[("ps", bank)], w=[("st_gate", si)])
                            P.dma(self.gate[rows, (cb - 8) * 512:(cb - 7) * 512], st_gate[si], r=[("st_gate", si)], w=[("gate", ti, tb, cb - 8)])

    def phase3(self):
        P = self.P
        qiT = self.alloc(4 * S, BF16, shape=(4, S))
        ksT = self.alloc(4 * S, BF16, shape=(4, S))
        kstok = self.alloc(16 * 512, BF16, shape=(16, 512))
        vh = self.alloc(16 * HV, BF16, shape=(16, HV))
        R = self.alloc(4 * HV, shape=(4, HV))
        Sbf2 = [self.alloc(4 * HV, BF16, shape=(4, HV)) for _ in range(2)]
        HS = S // 2
        gt2_ = [self.alloc(HS) for _ in range(2)]
        bt_2 = [self.alloc(HS) for _ in range(2)]
        bt2_2 = [self.alloc(HS) for _ in range(2)]
        ex_2 = [self.alloc(HS) for _ in range(2)]
        qk_2 = [self.alloc(HS) for _ in range(2)]
        ex2_2 = [self.alloc(HS) for _ in range(2)]
        qk2_2 = [self.alloc(HS) for _ in range(2)]
        pp_i = 0
        A = self.alloc(64, shape=(4, 16))
        rmask = self.alloc(S)
        tri = self.alloc(256, shape=(2, 128))
        sT = [self.alloc(128, BF16) for _ in range(2)]
        ostage = [self.alloc(HV) for _ in range(2)]
        NCH = S // 128
        P.dma(tri, self.c_tri.rearrange("d j i -> j d i"), w=["tri"])
        P.dve(lambda e: e.memset(rmask, 1.0), w=["rmask"])
        P.dve(lambda e: e.memset(rmask.rearrange("p (c t) -> p c t", t=128)[:, :, 0:1], 0.0), w=["rmask"])
        v3 = lambda ap: ap.rearrange("p (c t) -> p c t", t=128)
        kv_i = 0
        all_keys = lambda name: [(name, c) for c in range(4)]
        for hd in range(NH):
            P.dma(vh, self.v.rearrange("(n p) c -> p n c", p=128)[:, :, hd * HV:(hd + 1) * HV],
                  r=[("v", ti, tb, cb) for ti in self.tiles for tb in range(4) for cb in range(8)], w=["vh"])
            for d in range(2):
                for c in range(4):
                    fr = hd * HK + c * 128
                    for hf in range(2):
                        pi = pp_i % 2
                        pp_i += 1
                        gt, bt, bt2, ex, qk, ex2, qk2 = gt2_[pi], bt_2[pi], bt2_2[pi], ex_2[pi], qk_2[pi], ex2_2[pi], qk2_2[pi]
                        kk = lambda nm: (nm, pi)
                        ts_ = slice(hf * HS, (hf + 1) * HS)
                        NH2 = NCH // 2
                        P.dma(gt, self.gT[d, fr:fr + 128, ts_], w=[kk("gt")])
                        P.dve((lambda bt_, gt_, ts__: (lambda e: e.tensor_tensor_scan(out=bt_, data0=rmask[:, ts__], data1=gt_, initial=0.0, op0=ALU.mult, op1=ALU.add)))(bt, gt, ts_),
                              r=[kk("gt"), "rmask"], w=[kk("bt")])
                        if d == 0:
                            b_ = bt
                            bkey = kk("bt")
                            tot = v3(bt)[:, :, 127]
                        else:
                            P.dve((lambda bt_, gt_: (lambda e: e.tensor_tensor(out=gt_, in0=gt_, in1=bt_, op=ALU.subtract)))(bt, gt), r=[kk("gt"), kk("bt")], w=[kk("gt")])
                            P.dve((lambda bt_, gt_, b2_: (lambda e: e.tensor_tensor(out=v3(b2_), in0=v3(gt_), in1=v3(bt_)[:, :, 127:128].to_broadcast([128, NH2, 128]), op=ALU.add)))(bt, gt, bt2),
                                  r=[kk("gt"), kk("bt")], w=[kk("bt2")])
                            b_ = bt2
                            bkey = kk("bt2")
                            tot = v3(bt2)[:, :, 0]
                        P.act((lambda b__, ex_: (lambda e: e.activation(out=ex_, in_=b__, func=AF.Exp)))(b_, ex), r=[bkey], w=[kk("ex")])
                        P.dma(qk, self.qT[fr:fr + 128, ts_], w=[kk("qk")])
                        P.pool((lambda c_, qk_, ex_, ts__: (lambda e: e.tensor_tensor(out=qiT[:, c_, ts__], in0=qk_, in1=ex_, op=ALU.mult)))(c, qk, ex, ts_),
                               r=[kk("qk"), kk("ex")], w=[("qiT", c)])
                        P.act((lambda b__, ex_: (lambda e: e.activation(out=ex_, in_=b__, func=AF.Exp, scale=-1.0)))(b_, ex2), r=[bkey], w=[kk("ex2")])
                        P.dma(qk2, self.kT[fr:fr + 128, ts_], w=[kk("qk2")])
                        P.pool((lambda c_, qk_, ex_, ts__: (lambda e: e.tensor_tensor(out=ksT[:, c_, ts__], in0=qk_, in1=ex_, op=ALU.mult)))(c, qk2, ex2, ts_),
                               r=[kk("qk2"), kk("ex2")], w=[("ksT", c)])
                        P.act((lambda c_, tot_, hf_: (lambda e: e.activation(out=A[:, c_, hf_ * 8:(hf_ + 1) * 8], in_=tot_, func=AF.Exp)))(c, tot, hf), r=[bkey], w=[("A", c)])
                for n in range(NCH):
                    pb = 6 + (n % 2)
                    pt = self.ps[pb][:].bitcast(BF16)
                    for c in range(4):
                        P.pe((lambda c_, n_, pt_: (lambda e: e.transpose(pt_[:, c_ * 128:(c_ + 1) * 128], ksT[:, c_, n_ * 128:(n_ + 1) * 128], self.identb)))(c, n, pt),
                             r=[("ksT", c), "identb"], w=[("ps", pb)])
                    P.act((lambda n_, pt_: (lambda e: e.copy(out=kstok[:, n_, :], in_=pt_[:, 0:512])))(n, pt), r=[("ps", pb)], w=[("kstok", n)])
                order = list(range(NCH)) if d == 0 else list(range(NCH - 1, -1, -1))
                kvb = [2, 3, 4, 5]
                psc = self.ps[7]

                def emit_sc(step):
                    n = order[step]
                    nb = slice(n * 128, (n + 1) * 128)
                    for c in range(4):
                        P.pe((lambda c_, nb_: (lambda e: e.matmul(psc[:, 0:128], lhsT=ksT[:, c_, nb_], rhs=qiT[:, c_, nb_], start=(c_ == 0), stop=(c_ == 3))))(c, nb),
                             r=[("ksT", c), ("qiT", c)], w=[("ps", 7)])
                    sti = step % 2
                    P.dve((lambda sti_, d_: (lambda e: e.tensor_tensor(out=sT[sti_], in0=psc[:, 0:128], in1=tri[:, d_, :], op=ALU.mult)))(sti, d),
                          r=[("ps", 7), "tri"], w=[("sT", sti)])

                def emit_kv(step, idx):
                    nonlocal kv_i
                    n = order[step]
                    c, vv = idx // 2, idx % 2
                    vs = slice(vv * 512, (vv + 1) * 512)
                    bank = kvb[kv_i % 4]
                    kv_i += 1
                    pk = self.ps[bank]
                    P.pe((lambda c_, n_, pk_, vs_: (lambda e: e.matmul(pk_[:], lhsT=kstok[:, n_, c_ * 128:(c_ + 1) * 128], rhs=vh[:, n_, vs_], start=True, stop=True)))(c, n, pk, vs),
                         r=[("kstok", n), "vh"], w=[("ps", bank)])
                    return bank

                def emit_upd(step, idx, bank):
                    n = order[step]
                    c, vv = idx // 2, idx % 2
                    vs = slice(vv * 512, (vv + 1) * 512)
                    pk = self.ps[bank]
                    Snew = Sbf2[step % 2]
                    if step == 0:
                        P.dve((lambda c_, pk_, vs_: (lambda e: e.tensor_copy(out=R[:, c_, vs_], in_=pk_[:])))(c, pk, vs),
                              r=[("ps", bank)], w=[("R", c, vv)])
                    else:
                        npv = order[step - 1]
                        P.dve((lambda c_, pk_, vs_, np_: (lambda e: e.scalar_tensor_tensor(out=R[:, c_, vs_], in0=R[:, c_, vs_], scalar=A[:, c_, np_:np_ + 1], in1=pk_[:],
                                                                                          op0=ALU.mult, op1=ALU.add)))(c, pk, vs, npv),
                              r=[("ps", bank), ("R", c, vv), ("A", c)], w=[("R", c, vv)])
                    P.act((lambda c_, vs_, n_, Sn_: (lambda e: e.activation(out=Sn_[:, c_, vs_], in_=R[:, c_, vs_], func=AF.Copy, scale=A[:, c_, n_:n_ + 1])))(c, vs, n, Snew),
                          r=[("R", c, vv), ("A", c)], w=[("Sbf", step % 2, c, vv)])

                def emit_o(step):
                    n = order[step]
                    nb = slice(n * 128, (n + 1) * 128)
                    sti = step % 2
                    Sprev = Sbf2[(step + 1) % 2]
                    for vv in range(2):
                        po = self.ps[vv]
                        vs = slice(vv * 512, (vv + 1) * 512)
                        if step > 0:
                            for c in range(4):
                                P.pe((lambda c_, nb_, po_, vs_, Sp_: (lambda e: e.matmul(po_[:], lhsT=qiT[:, c_, nb_], rhs=Sp_[:, c_, vs_], start=(c_ == 0), stop=False)))(c, nb, po, vs, Sprev),
                                     r=[("qiT", c), ("Sbf", (step + 1) % 2, c, vv)], w=[("ps", vv)])
                        P.pe((lambda sti_, n_, po_, vs_, st_: (lambda e: e.matmul(po_[:], lhsT=sT[sti_], rhs=vh[:, n_, vs_], start=st_, stop=True)))(sti, n, po, vs, step == 0),
                             r=[("sT", sti), "vh"], w=[("ps", vv)])
                        P.act((lambda sti_, po_, vs_: (lambda e: e.copy(out=ostage[sti_][:, vs_], in_=po_[:])))(sti, po, vs),
                              r=[("ps", vv)], w=[("ostage", sti)])
                    P.dma(self.ofb[d, n * 128:(n + 1) * 128, hd * HV:(hd + 1) * HV], ostage[sti], r=[("ostage", sti)], w=[("ofb", d, n, hd)])

                emit_sc(0)
                pre = [emit_kv(0, i) for i in range(4)]
                for step in range(NCH):
                    if step < NCH - 1:
                        banks = list(pre)
                        for i in range(4):
                            emit_upd(step, i, banks[i])
                            banks.append(emit_kv(step, 4 + i))
                        emit_o(step)
                        for i in range(4, 8):
                            emit_upd(step, i, banks[i])
                    else:
                        emit_o(step)
                    if step + 1 < NCH:
                        emit_sc(step + 1)
                        if step + 1 < NCH - 1:
                            pre = [emit_kv(step + 1, i) for i in range(4)]

    def phase4(self):
        P = self.P
        self.h = self.alloc(4 * D, shape=(4, D))
        self.hnT = self.alloc(KC * 512, BF16, shape=(KC, 512))
        self.xnb = self.alloc(D, BF16)
        self.alloc_mlp()
        hnT, WA = self.hnT, self.WA
        bcB = self.alloc(HV)
        of2 = [self.alloc(HV) for _ in range(2)]
        ob2 = [self.alloc(HV) for _ in range(2)]
        gt2 = [self.alloc(HV) for _ in range(2)]
        og2 = [self.alloc(HV, BF16) for _ in range(2)]
        pro_i = 0
        rstd = self.alloc(8)
        rstd_o = self.alloc(8)
        self.tr_banks = [4, 5]
        self.mm_banks = [0, 1, 2, 3]
        self.tr_i = 0
        self.mm_i = 0
        wout = self.w_out.rearrange("(kc p) n -> p kc n", p=128)
        for ti in self.tiles:
            t0 = ti * T
            for tb in range(4):
                P.dma(self.h[:, tb, :], self.h1[t0 + tb * 128:t0 + (tb + 1) * 128, :], r=[("h1", ti, tb)], w=[("h", tb)])
            P.dma(bcB, self.g_norm[0:1, :].partition_broadcast(128), w=["bcB"])
            for tb in range(4):
                rows = slice(t0 + tb * 128, t0 + (tb + 1) * 128)
                n = ti * 4 + tb
                for hd in range(NH):
                    cs = slice(hd * HV, (hd + 1) * HV)
                    bi = pro_i % 2
                    pro_i += 1
                    of_, ob_, gt_, og = of2[bi], ob2[bi], gt2[bi], og2[bi]
                    kof, kob, kgt, kog, krs = ("of_", bi), ("ob_", bi), ("gt_", bi), ("og", bi), ("rstd_o", bi)
                    rso = rstd_o[:, bi:bi + 1]
                    P.dma(of_, self.ofb[0, rows, cs], r=[("ofb", 0, n, hd)], w=[kof])
                    P.dma(ob_, self.ofb[1, rows, cs], r=[("ofb", 1, n, hd)], w=[kob])
                    P.dma(gt_, self.gate[rows, cs], r=[("gate", ti, tb, cb) for cb in (2 * hd, 2 * hd + 1)], w=[kgt])
                    P.dve((lambda of__, ob__: (lambda e: e.tensor_tensor(out=of__, in0=of__, in1=ob__, op=ALU.add)))(of_, ob_), r=[kof, kob], w=[kof])
                    self.rstd_of(of_, rso, ob_, [kof], HV, krs, junk_key=kob)
                    P.act((lambda gt__: (lambda e: e.activation(out=gt__, in_=gt__, func=AF.Silu)))(gt_), r=[kgt], w=[kgt])
                    P.dve((lambda of__, rso_: (lambda e: e.scalar_tensor_tensor(out=of__, in0=of__, scalar=rso_, in1=bcB, op0=ALU.mult, op1=ALU.mult)))(of_, rso),
                          r=[kof, krs, "bcB"], w=[kof])
                    P.dve((lambda og_, of__, gt__: (lambda e: e.tensor_tensor(out=og_, in0=of__, in1=gt__, op=ALU.mult)))(og, of_, gt_), r=[kof, kgt], w=[kog])
                    self.transpose_into_hnT(og, kog, tb, None, kc0=hd * 8, nkc=8)
            for ob in range(16):
                wi = self.wa_i % 2
                self.wa_i += 1
                P.dma(self.WA[wi], wout[:, :, ob * 256:(ob + 1) * 256], w=[("WA", wi)], q="pool")
                for tb in range(4):
                    bank = self.next_mm_bank()
                    pm = self.ps[bank]
                    for kc in range(KC):
                        P.pe((lambda kc_, tb_, wi_, pm_: (lambda e: e.matmul(pm_[:, 0:256], lhsT=hnT[:, kc_, tb_ * 128:(tb_ + 1) * 128], rhs=WA[wi_][:, kc_, :],
                                                                              start=(kc_ == 0), stop=(kc_ == KC - 1))))(kc, tb, wi, pm),
                             r=[("WA", wi), ("hnT", tb)], w=[("ps", bank)])
                    hsl = self.h[:, tb, ob * 256:(ob + 1) * 256]
                    P.dve((lambda hs_, pm_: (lambda e: e.tensor_tensor(out=hs_, in0=hs_, in1=pm_[:, 0:256], op=ALU.add)))(hsl, pm),
                          r=[("ps", bank), ("h", tb)], w=[("h", tb)])
            for tb in range(4):
                self.norm_transpose(self.h[:, tb, :], [("h", tb)], tb, 2, self.xnb, rstd[:, tb:tb + 1], ("rstd", tb))
            self.mlp(1, self.tiles.index(ti), len(self.tiles))
            for tb in range(4):
                self.rstd_of(self.h[:, tb, :], rstd[:, tb:tb + 1], self.xnb, [("h", tb)], D, ("rstd", tb))
            for pc in range(4):
                P.dma(bcB, self.norm_final[0:1, pc * 1024:(pc + 1) * 1024].partition_broadcast(128), w=["bcB"])
                for tb in range(4):
                    hsl = self.h[:, tb, pc * 1024:(pc + 1) * 1024]
                    P.dve((lambda hs_, tb_: (lambda e: e.scalar_tensor_tensor(out=hs_, in0=hs_, scalar=rstd[:, tb_:tb_ + 1], in1=bcB, op0=ALU.mult, op1=ALU.mult)))(hsl, tb),
                          r=[("h", tb), ("rstd", tb), "bcB"], w=[("h", tb)])
            for tb in range(4):
                P.dma(self.out[t0 + tb * 128:t0 + (tb + 1) * 128, :], self.h[:, tb, :], r=[("h", tb)], w=[("out", ti, tb)])


def build_program(**kw):
    b = Builder(**kw)
    return b.build(), b


def host_inputs(names, batch, inp, consts):
    m = {
        "x": lambda: inp["x"][batch],
        "norm_mix": lambda: inp["norm_mix"],
        "norm_mlp": lambda: inp["norm_mlp"],
        "norm_final": lambda: inp["norm_final"].reshape(1, D),
        "pool_w": lambda: inp["pool_w"][0],
        "pool_scale": lambda: inp["pool_scale"].reshape(1, D),
        "gla_w_in": lambda: inp["gla_w_in"][0],
        "gla_w_up_f": lambda: inp["gla_w_up_f"][0],
        "gla_w_up_b": lambda: inp["gla_w_up_b"][0],
        "gla_b_up_f": lambda: inp["gla_b_up_f"].reshape(1, 2048),
        "gla_b_up_b": lambda: inp["gla_b_up_b"].reshape(1, 2048),
        "gla_g_norm": lambda: inp["gla_g_norm"].reshape(1, HV),
        "gla_w_out": lambda: inp["gla_w_out"][0],
        "mlp_w_in": lambda: inp["mlp_w_in"],
        "mlp_w_out": lambda: inp["mlp_w_out"],
    }
    out = {}
    for n in names:
        if n in consts:
            out[n] = consts[n]
        elif n in m:
            out[n] = np.ascontiguousarray(m[n]())
        else:
            out[n] = np.ascontiguousarray(inp[n])
    return out


def kernel(**inputs):
    inputs = {k: np.asarray(v) for k, v in inputs.items()}
    nc, b = build_program()
    consts = make_consts()
    n = 8
    in_maps = [host_inputs(b.in_names, i, inputs, consts) for i in range(n)]
    res = run_bass_kernel_spmd(nc, in_maps, core_ids=list(range(n)))
    return np.stack([np.asarray(r["out"]) for r in res.results]).astype(np.float32)
```

```python
import numpy as np
from contextlib import ExitStack
import concourse.bass as bass
import concourse.mybir as mybir
from concourse.bass_utils import run_bass_kernel_spmd

F32 = mybir.dt.float32
BF16 = mybir.dt.bfloat16
AF = mybir.ActivationFunctionType
ALU = mybir.AluOpType

ENGS = ("pe", "act", "dve", "pool", "sp")

S = 2048
D = 4096
DFF = 16384
T = 512
NT = S // T
KC = D // 128
EPS = 1e-6
POOL_WINDOWS = (2, 4, 8, 16)
HK = 512
HV = 1024
NH = 4
PROJ = 12320


class Op:
    __slots__ = ("eng", "fn", "reads", "writes", "dma", "idx", "waits", "signal", "mile", "tok", "bar")

    def __init__(self, eng, fn, reads, writes, dma):
        self.eng = eng
        self.fn = fn
        self.reads = reads
        self.writes = writes
        self.dma = dma
        self.waits = []
        self.signal = False
        self.mile = 0
        self.tok = None
        self.bar = None


class Prog:
    def __init__(self, nc, dma_slots=None):
        self.nc = nc
        self.ops = []
        self.dma_slots = dma_slots or {"sp": 8, "pool": 4, "act": 2}

    def op(self, eng, fn, reads=(), writes=(), dma=False):
        o = Op(eng, fn, tuple(reads), tuple(writes), dma)
        o.idx = len(self.ops)
        self.ops.append(o)
        return o

    def pe(self, fn, r=(), w=()):
        return self.op("pe", fn, r, w)

    def act(self, fn, r=(), w=()):
        return self.op("act", fn, r, w)

    def dve(self, fn, r=(), w=()):
        return self.op("dve", fn, r, w)

    def pool(self, fn, r=(), w=()):
        return self.op("pool", fn, r, w)

    def dma(self, out, in_, r=(), w=(), q="sp", **kw):
        return self.op(q, lambda e: e.dma_start(out=out, in_=in_, **kw), r, w, dma=True)

    def barrier(self, scratch):
        bid = len(self.ops)
        marks = {}
        for e, s in scratch.items():
            if e == "pe":
                continue
            o = self.op(e, (lambda s_, e_: (lambda en: en.memzero(s_) if e_ == 'act' else en.memset(s_, 0.0)))(s, e), (), ())
            o.bar = ("mark", bid)
            marks[e] = o
        for e in ENGS:
            o = self.op(e, None, (), ())
            o.bar = ("join", bid)

    def finalize(self):
        ops = self.ops
        last_w = {}
        readers = {}
        known = {e: {} for e in ENGS}
        known_dma = {e: set() for e in ENGS}
        last_sig = {e: None for e in ENGS}
        out_dma = {e: [] for e in ENGS}
        for o in ops:
            if o.bar is not None and o.bar[0] == "join":
                for pe_ in ("pe", "act", "dve", "pool"):
                    d = last_sig[pe_]
                    if d is None or pe_ == o.eng:
                        continue
                    if known[o.eng].get(pe_, -1) >= d:
                        continue
                    known[o.eng][pe_] = d
                    o.waits.append(("eng", pe_, d))
                    ops[d].signal = True
                for q in ENGS:
                    k = self.dma_slots.get(q, 0)
                    for d in out_dma[q][-k:] if k else []:
                        if d not in known_dma[o.eng]:
                            known_dma[o.eng].add(d)
                            o.waits.append(("dma", d))
                continue
            raw = set()
            oth = set()
            for r in o.reads:
                lw = last_w.get(r)
                if lw is not None:
                    raw.add(lw)
            for w in o.writes:
                lw = last_w.get(w)
                if lw is not None:
                    oth.add(lw)
                for rd in readers.get(w, ()):
                    oth.add(rd)
            for r in o.reads:
                readers.setdefault(r, []).append(o.idx)
            for w in o.writes:
                last_w[w] = o.idx
                readers[w] = []
            need_eng = {}
            for d in raw | oth:
                if d == o.idx:
                    continue
                p = ops[d]
                if p.dma:
                    if d not in known_dma[o.eng]:
                        o.waits.append(("dma", d))
                        known_dma[o.eng].add(d)
                    continue
                if p.eng == o.eng:
                    if o.eng == "pe" or d not in raw:
                        continue
                need_eng[p.eng] = max(need_eng.get(p.eng, -1), d)
            for pe_, d in need_eng.items():
                if known[o.eng].get(pe_, -1) >= d:
                    continue
                known[o.eng][pe_] = d
                o.waits.append(("eng", pe_, d))
                ops[d].signal = True
            if o.dma:
                out_dma[o.eng].append(o.idx)
            elif o.eng != "sp":
                last_sig[o.eng] = o.idx
        cnt = {e: 0 for e in ENGS}
        dcnt = {e: 0 for e in ENGS}
        for o in ops:
            if o.dma:
                k = self.dma_slots[o.eng]
                n = dcnt[o.eng]
                o.tok = (n % k, 16 * (n // k + 1), n)
                dcnt[o.eng] += 1
            elif o.signal:
                cnt[o.eng] += 1
                o.mile = cnt[o.eng]
        self.n_signal = cnt
        self.n_dma = dcnt

    def emit(self):
        nc = self.nc
        self.finalize()
        ops = self.ops
        with ExitStack() as es:
            esem = {e: es.enter_context(nc.semaphore("s_" + e)) for e in ENGS}
            dsem = {
                q: [es.enter_context(nc.semaphore("d_%s%d" % (q, i))) for i in range(k)]
                for q, k in self.dma_slots.items()
            }
            block = es.enter_context(nc.Block())
            per = {e: [o for o in ops if o.eng == e] for e in ENGS}

            def run(engname, eng):
                k = self.dma_slots.get(engname, 0)
                my_dmas = [o for o in per[engname] if o.dma]
                for o in per[engname]:
                    for wt in o.waits:
                        if wt[0] == "eng":
                            p = ops[wt[2]]
                            eng.wait_ge(esem[p.eng], p.mile)
                        else:
                            p = ops[wt[1]]
                            eng.wait_ge(dsem[p.eng][p.tok[0]], p.tok[1])
                    if o.fn is None:
                        continue
                    if o.dma:
                        slot, val, n = o.tok
                        if n >= k:
                            eng.wait_ge(dsem[engname][slot], val - 16)
                        o.fn(eng).then_inc(dsem[engname][slot], 16)
                    else:
                        ins = o.fn(eng)
                        if o.signal:
                            ins.then_inc(esem[engname], 1)
                if my_dmas:
                    last = {}
                    for o in my_dmas:
                        last[o.tok[0]] = o.tok[1]
                    for slot, val in last.items():
                        eng.wait_ge(dsem[engname][slot], val)

            @block.sync
            def _(e):
                run("sp", e)

            @block.tensor
            def _(e):
                run("pe", e)

            @block.scalar
            def _(e):
                run("act", e)

            @block.vector
            def _(e):
                run("dve", e)

            @block.gpsimd
            def _(e):
                run("pool", e)


def _pool_A(i, j, w):
    lo = max(i - w // 2, 0)
    hi = min(i + w // 2, S)
    return (1.0 / (hi - lo)) if lo <= j < hi else 0.0


def make_consts():
    bands = np.zeros((4, 5, 128, 128), np.float32)
    for g, w in enumerate(POOL_WINDOWS):
        def blk(I, J):
            m = np.zeros((128, 128), np.float32)
            for ii in range(128):
                i = I * 128 + ii
                for j in range(max(i - 16, J * 128), min(i + 17, (J + 1) * 128)):
                    if j < 0 or j >= S:
                        continue
                    m[j - J * 128, ii] = _pool_A(i, j, w) - (1.0 if i == j else 0.0)
            return m
        bands[g, 0] = blk(5, 5)
        bands[g, 1] = blk(5, 4)
        bands[g, 2] = blk(5, 6)
        bands[g, 3] = blk(0, 0)
        bands[g, 4] = blk(15, 15)
    ident = np.eye(128, dtype=np.float32)
    jj = np.arange(128)[:, None]
    ii = np.arange(128)[None, :]
    tri = np.stack([(ii >= jj), (ii <= jj)]).astype(np.float32)
    return {"c_bands": bands, "c_ident": ident, "c_tri": tri}


class Builder:
    def __init__(self, phases=(1, 2, 3, 4), tiles=None, ext=()):
        self.phases = phases
        self.tiles = list(range(NT)) if tiles is None else list(tiles)
        self.ext = set(ext)
        self.nc = bass.Bass("TRN2", target_bir_lowering=False)
        self.P = Prog(self.nc)
        self.in_names = []

    def din(self, name, shape, dt=F32, ph=(1, 2, 3, 4)):
        if not any(p in self.phases for p in ph):
            return None
        self.in_names.append(name)
        return self.nc.dram_tensor(name, list(shape), dt, kind="ExternalInput").ap()

    def dscr(self, name, shape, dt, produced_by):
        if name in self.ext:
            kind = "ExternalOutput" if produced_by in self.phases else "ExternalInput"
        else:
            kind = "Internal"
        return self.nc.dram_tensor(name, list(shape), dt, kind=kind).ap()

    def reset_arena(self):
        self.off = self.persist_off

    def alloc(self, nfree, dt=F32, parts=128, shape=None):
        nbytes = nfree * (4 if dt == F32 else 2)
        nw = (nbytes + 3) // 4
        nw = (nw + 7) // 8 * 8
        assert self.off + nw <= self.arena_words, ("SBUF arena overflow", self.off, nw, self.arena_words)
        v = self.arena[0:parts, self.off:self.off + nw]
        self.off += nw
        if dt != F32:
            v = v.bitcast(dt)
        v = v[:, 0:nfree]
        if shape is not None:
            assert len(shape) == 2
            v = v.rearrange("p (a b) -> p a b", a=shape[0], b=shape[1])
        return v

    def build(self):
        nc = self.nc
        P = self.P
        with ExitStack() as es:
            self.arena_words = 53200
            self.arena = es.enter_context(nc.sbuf_tensor("arena", [128, self.arena_words], F32))
            self.ps = [es.enter_context(nc.psum_tensor("ps%d" % i, [128, 512], F32)) for i in range(8)]
            self.off = 0
            self.declare_dram()
            self.setup_consts()
            self.persist_off = self.off
            first = True
            for ph in (1, 2, 3, 4):
                if ph not in self.phases:
                    continue
                if not first:
                    P.barrier(self.bar_scratch)
                first = False
                self.reset_arena()
                getattr(self, "phase%d" % ph)()
            P.emit()
        return nc

    def declare_dram(self):
        self.x = self.din("x", [S, D], ph=(1,))
        self.norm_mix = self.din("norm_mix", [2, D])
        self.norm_mlp = self.din("norm_mlp", [2, D])
        self.norm_final = self.din("norm_final", [1, D], ph=(4,))
        self.pool_w = self.din("pool_w", [4, 1024, 1024], ph=(1,))
        self.pool_scale = self.din("pool_scale", [1, D], ph=(1,))
        self.w_in = self.din("gla_w_in", [D, PROJ], ph=(2,))
        self.w_up = [self.din("gla_w_up_f", [16, 2048], ph=(2,)), self.din("gla_w_up_b", [16, 2048], ph=(2,))]
        self.b_up = [self.din("gla_b_up_f", [1, 2048], ph=(2,)), self.din("gla_b_up_b", [1, 2048], ph=(2,))]
        self.g_norm = self.din("gla_g_norm", [1, HV], ph=(4,))
        self.w_out = self.din("gla_w_out", [D, D], ph=(4,))
        self.mlp_w1 = self.din("mlp_w_in", [2, D, DFF], ph=(1, 3, 4))
        self.mlp_w2 = self.din("mlp_w_out", [2, DFF, D], ph=(1, 3, 4))
        self.c_bands = self.din("c_bands", [4, 5, 128, 128], ph=(1,))
        self.c_ident = self.din("c_ident", [128, 128])
        self.c_tri = self.din("c_tri", [2, 128, 128], ph=(3,))
        self.h1 = self.dscr("h1", [S, D], F32, 1)
        self.qT = self.dscr("qT", [2048, S], F32, 2)
        self.kT = self.dscr("kT", [2048, S], F32, 2)
        self.gT = self.dscr("gT", [2, 2048, S], F32, 2)
        self.v = self.dscr("v", [S, D], BF16, 2)
        self.gate = self.dscr("gate", [S, D], F32, 2)
        self.ofb = self.dscr("ofb", [2, S, D], F32, 3)
        self.w1s = self.nc.dram_tensor("w1s", [2, 64, 128, KC * 256], BF16, kind="Internal").ap()
        self.w2s = self.nc.dram_tensor("w2s", [2, 128, 128, 8 * 512], BF16, kind="Internal").ap()

        self.out = self.nc.dram_tensor("out", [S, D], F32, kind="ExternalOutput").ap()

    def setup_consts(self):
        P = self.P
        self.identf = self.alloc(128)
        self.identb = self.alloc(128, BF16)
        self.gcol = self.alloc(4 * KC, shape=(4, KC))
        self.bscr = {e: self.alloc(8) for e in ("act", "dve", "pool")}
        self.bar_scratch = {e: v[:, 0:1] for e, v in self.bscr.items()}
        P.dma(self.identf, self.c_ident, w=["identf"])
        P.dve(lambda e: e.tensor_copy(out=self.identb, in_=self.identf), r=["identf"], w=["identb"])
        srcs = [self.norm_mlp[0:1, :], self.norm_mix[1:2, :], self.norm_mlp[1:2, :]]
        for n, src in enumerate(srcs):
            P.dma(self.gcol[:, n, :], src.rearrange("o (kc p) -> p (o kc)", p=128), w=["gcol"],
                  allow_slow_non_contiguous=True)

    def rstd_of(self, src, rstd_ap, junk, keys_r, n_elems, key_w, junk_key="xnb"):
        P = self.P
        P.pool(lambda e: e.memset(rstd_ap, 0.0), w=[key_w])
        jk = list(junk_key) if isinstance(junk_key, list) else [junk_key]
        P.act(lambda e: e.activation(out=junk, in_=src, func=AF.Square, accum_out=rstd_ap),
              r=keys_r, w=[key_w] + jk)
        P.act(lambda e: e.activation(out=rstd_ap, in_=rstd_ap, func=AF.Sqrt, bias=EPS, scale=1.0 / n_elems),
              r=[key_w], w=[key_w])
        P.dve(lambda e: e.reciprocal(out=rstd_ap, in_=rstd_ap), r=[key_w], w=[key_w])

    def norm_transpose(self, hsrc, hkeys, tb, gidx, xnb, rstd_ap, rkey, xkeys=None, junk=None, junk_keys=None):
        P = self.P
        xkeys = ["xnb"] if xkeys is None else list(xkeys)
        if junk is None:
            junk, junk_keys = xnb, xkeys
        self.rstd_of(hsrc, rstd_ap, junk, hkeys, D, rkey, junk_key=list(junk_keys))
        P.act(lambda e: e.activation(out=xnb, in_=hsrc, func=AF.Copy, scale=rstd_ap), r=list(hkeys) + [rkey], w=xkeys)
        self.transpose_into_hnT(xnb, xkeys, tb, gidx)

    def transpose_into_hnT(self, src_bf, skey, tb, gidx, kc0=0, nkc=KC):
        P = self.P
        for g8 in range(0, nkc, 8):
            bank = self.tr_banks[self.tr_i % len(self.tr_banks)]
            self.tr_i += 1
            pt = self.ps[bank][:].bitcast(BF16)
            for j in range(8):
                kc = g8 + j
                P.pe((lambda kc_, j_, pt_: (lambda e: e.transpose(pt_[:, j_ * 128:(j_ + 1) * 128],
                                                                     src_bf[:, kc_ * 128:(kc_ + 1) * 128], self.identb)))(kc, j, pt),
                     r=(list(skey) if isinstance(skey, list) else [skey]) + ["identb"], w=[("ps", bank)])
            dst = self.hnT[:, kc0 + g8:kc0 + g8 + 8, tb * 128:(tb + 1) * 128]
            srcv = pt.rearrange("p (a b) -> p a b", a=8, b=128)
            if gidx is None:
                P.act((lambda d_, s_: (lambda e: e.copy(out=d_, in_=s_)))(dst, srcv), r=[("ps", bank)], w=[("hnT", tb)])
            else:
                gc = self.gcol[:, gidx, kc0 + g8:kc0 + g8 + 8]
                P.dve((lambda d_, s_, g_: (lambda e: e.tensor_tensor(out=d_, in0=s_, in1=g_.unsqueeze(2).to_broadcast([128, 8, 128]), op=ALU.mult)))(dst, srcv, gc),
                      r=[("ps", bank), "gcol"], w=[("hnT", tb)])

    def next_mm_bank(self):
        b = self.mm_banks[self.mm_i % len(self.mm_banks)]
        self.mm_i += 1
        return b

    def alloc_mlp(self):
        self.WA = [self.alloc(KC * 256, BF16, shape=(KC, 256)) for _ in range(2)]
        self.WB = [self.alloc(8 * 512, BF16, shape=(8, 512)) for _ in range(2)]
        self.aT = [self.alloc(8 * 512, BF16, shape=(8, 512)) for _ in range(2)]
        self.rtmp = [self.alloc(512) for _ in range(2)]
        self.WB = self.WB + [self.xnb.rearrange("p (a b) -> p a b", a=8, b=512)]
        self.WBk = [("WB", 0), ("WB", 1), "xnb"]
        self.wa_i = 0
        self.wb_i = 0
        self.rt_i = 0

    def w1_src(self, layer, piece):
        w1 = self.mlp_w1[layer].rearrange("(kc p) f -> p kc f", p=128)
        return w1[:, :, piece * 256:(piece + 1) * 256]

    def w2_src(self, layer, piece):
        w2 = self.mlp_w2[layer].rearrange("(fc p) n -> p fc n", p=128)
        bb, ob = piece // 8, piece % 8
        return w2[:, bb * 8:(bb + 1) * 8, ob * 512:(ob + 1) * 512]

    def w1_scr(self, layer, piece):
        return self.w1s[layer, piece].rearrange("p (kc j) -> p kc j", kc=KC)

    def w2_scr(self, layer, piece):
        return self.w2s[layer, piece].rearrange("p (f j) -> p f j", f=8)

    def convert_mlp_weights(self, layer):
        P = self.P
        for b in range(DFF // 1024):
            for qtr in range(4):
                pc = b * 4 + qtr
                P.dma(self.w1_scr(layer, pc), self.w1_src(layer, pc), w=[("w1s", layer, pc)], q="pool")
            for ob in range(8):
                pc = b * 8 + ob
                P.dma(self.w2_scr(layer, pc), self.w2_src(layer, pc), w=[("w2s", layer, pc)], q="pool")

    def mlp(self, layer, tidx, ntiles):
        P = self.P
        WA, WB, aT, hnT = self.WA, self.WB, self.aT, self.hnT
        NB = DFF // 1024
        hn_keys = [("hnT", tb) for tb in range(4)]

        def load(buf, bkey, scr, src, skey, kind, pc):
            if kind == 1:
                store_at = 0
            else:
                store_at = 1 if pc % 2 == 0 else 2
            store_at = min(store_at, ntiles - 1)
            if tidx > store_at:
                P.dma(buf, scr, r=[skey], w=[bkey], q="sp")
            else:
                P.dma(buf, src, w=[bkey], q="pool")
                if tidx == store_at and ntiles > 1:
                    P.dma(scr, buf, r=[bkey], w=[skey], q="sp")

        for b in range(NB + 1):
            for qtr in range(4):
                if b < NB:
                    wi = self.wa_i % 2
                    self.wa_i += 1
                    pc = b * 4 + qtr
                    load(WA[wi], ("WA", wi), self.w1_scr(layer, pc), self.w1_src(layer, pc), ("w1s", layer, pc), 1, pc)
                    for cc in range(2):
                        fl = qtr * 2 + cc
                        bank = self.next_mm_bank()
                        pm = self.ps[bank]
                        for kc in range(KC):
                            P.pe((lambda kc_, cc_, wi_, pm_: (lambda e: e.matmul(pm_[:], lhsT=WA[wi_][:, kc_, cc_ * 128:(cc_ + 1) * 128],
                                                                                  rhs=hnT[:, kc_, :], start=(kc_ == 0), stop=(kc_ == KC - 1))))(kc, cc, wi, pm),
                                 r=[("WA", wi)] + hn_keys, w=[("ps", bank)])
                        ri = self.rt_i % 2
                        self.rt_i += 1
                        rt = self.rtmp[ri]
                        P.act((lambda rt_, pm_: (lambda e: e.activation(out=rt_, in_=pm_[:], func=AF.Relu)))(rt, pm),
                              r=[("ps", bank)], w=[("rtmp", ri)])
                        P.dve((lambda rt_, b_, fl_: (lambda e: e.tensor_tensor(out=aT[b_ % 2][:, fl_, :], in0=rt_, in1=rt_, op=ALU.mult)))(rt, b, fl),
                              r=[("rtmp", ri)], w=[("aT", b % 2, fl)])
                if b > 0:
                    bb = b - 1
                    for ob in (2 * qtr, 2 * qtr + 1):
                        wi = self.wb_i % 3
                        self.wb_i += 1
                        pc = bb * 8 + ob
                        load(WB[wi], self.WBk[wi], self.w2_scr(layer, pc), self.w2_src(layer, pc), ("w2s", layer, pc), 2, pc)
                        for tb in range(4):
                            bank = self.next_mm_bank()
                            pm = self.ps[bank]
                            for f in range(8):
                                P.pe((lambda f_, tb_, wi_, pm_, bb_: (lambda e: e.matmul(pm_[:], lhsT=aT[bb_ % 2][:, f_, tb_ * 128:(tb_ + 1) * 128],
                                                                                         rhs=WB[wi_][:, f_, :], start=(f_ == 0), stop=(f_ == 7))))(f, tb, wi, pm, bb),
                                     r=[self.WBk[wi], ("aT", bb % 2, f)], w=[("ps", bank)])
                            hs = self.h[:, tb, ob * 512:(ob + 1) * 512]
                            P.dve((lambda hs_, pm_: (lambda e: e.tensor_tensor(out=hs_, in0=hs_, in1=pm_[:], op=ALU.add)))(hs, pm),
                                  r=[("ps", bank), ("h", tb)], w=[("h", tb)])

    def phase1(self):
        P = self.P
        self.h = self.alloc(4 * D, shape=(4, D))
        self.hnT = self.alloc(KC * 512, BF16, shape=(KC, 512))
        self.xnb = self.alloc(D, BF16)
        self.alloc_mlp()
        h_, WB_ = self.h, self.WB
        bands = self.alloc(20 * 128, BF16, shape=(20, 128))
        bcB = self.alloc(1024)
        bcC = self.alloc(512)
        xnt = self.alloc(512)
        xhi = self.alloc(4 * 512, BF16, shape=(4, 512))
        xlo = self.alloc(4 * 512, BF16, shape=(4, 512))
        hhi = self.alloc(2 * 512, BF16, shape=(2, 512))
        hlo = self.alloc(2 * 512, BF16, shape=(2, 512))
        rstd = self.alloc(8)
        rstd_h = self.alloc(8)
        ssq_h = self.alloc(16, shape=(2, 8))
        rkeep = self.alloc(8)
        prev_ti = None
        self.tr_banks = [4, 5]
        self.mm_banks = [0, 1, 2, 3]
        dps_banks = [6, 7]
        self.tr_i = 0
        self.mm_i = 0
        dps_i = 0
        P.dma(bands, self.c_bands.rearrange("g k j i -> j (g k) i"), w=["bands"], q="pool")
        dT = self.aT[0]
        tmpA = self.rtmp[0]
        tmpB = self.rtmp[1]
        for ti in self.tiles:
            t0 = ti * T
            for tb in range(4):
                P.dma(self.h[:, tb, :], self.x[t0 + tb * 128:t0 + (tb + 1) * 128, :], w=[("h", tb)])
            for tb in range(4):
                self.rstd_of(self.h[:, tb, :], rstd[:, tb:tb + 1], self.xnb, [("h", tb)], D, ("rstd", tb))
            hp = []
            if ti > 0:
                hp.append((0, t0 - 128))
            if ti < NT - 1:
                hp.append((1, t0 + T))
            xf = self.xnb.bitcast(F32)
            for (sl, r0) in hp:
                if sl == 0 and prev_ti == ti - 1:
                    P.dve((lambda c_: (lambda e: e.tensor_copy(out=rstd_h[:, 0:1], in_=rkeep[:, c_:c_ + 1])))((ti - 1) % 2),
                          r=[("rkeep", (ti - 1) % 2)], w=["rstd_h"])
                    continue
                P.pool((lambda sl_: (lambda e: e.memset(ssq_h[:, sl_, :], 0.0)))(sl), w=["ssq_h"])
                for pc in range(2):
                    P.dma(xf, self.x[r0:r0 + 128, pc * 2048:(pc + 1) * 2048], w=["xnb"])
                    P.act((lambda sl_, pc_: (lambda e: e.activation(out=xf, in_=xf, func=AF.Square, accum_out=ssq_h[:, sl_, pc_:pc_ + 1])))(sl, pc),
                          r=["xnb"], w=["xnb", "ssq_h"])
                P.dve((lambda sl_: (lambda e: e.tensor_reduce(out=rstd_h[:, sl_:sl_ + 1], in_=ssq_h[:, sl_, 0:2], axis=mybir.AxisListType.X, op=ALU.add)))(sl),
                      r=["ssq_h"], w=["rstd_h"])
                P.act((lambda sl_: (lambda e: e.activation(out=rstd_h[:, sl_:sl_ + 1], in_=rstd_h[:, sl_:sl_ + 1], func=AF.Sqrt, bias=EPS, scale=1.0 / D)))(sl),
                      r=["rstd_h"], w=["rstd_h"])
                P.dve((lambda sl_: (lambda e: e.reciprocal(out=rstd_h[:, sl_:sl_ + 1], in_=rstd_h[:, sl_:sl_ + 1])))(sl), r=["rstd_h"], w=["rstd_h"])
            P.dve((lambda c_: (lambda e: e.tensor_copy(out=rkeep[:, c_:c_ + 1], in_=rstd[:, 3:4])))(ti % 2), r=[("rstd", 3)], w=[("rkeep", ti % 2)])
            prev_ti = ti
            hslots = dict(hp)
            for g in range(4):
                P.dma(bcB, self.pool_scale[0:1, g * 1024:(g + 1) * 1024].partition_broadcast(128), w=["bcB"])
                for hh in range(2):
                    c0 = g * 1024 + hh * 512
                    P.dma(bcC, self.norm_mix[0:1, c0:c0 + 512].partition_broadcast(128), w=["bcC"])
                    for tb in range(4):
                        P.dve((lambda tb_, c0_: (lambda e: e.scalar_tensor_tensor(out=xnt, in0=h_[:, tb_, c0_:c0_ + 512], scalar=rstd[:, tb_:tb_ + 1],
                                                                                   in1=bcC, op0=ALU.mult, op1=ALU.mult)))(tb, c0),
                              r=[("h", tb), ("rstd", tb), "bcC"], w=["xnt"])
                        P.dve((lambda tb_: (lambda e: e.tensor_copy(out=xhi[:, tb_, :], in_=xnt)))(tb), r=["xnt"], w=[("xhi", tb)])
                        P.dve((lambda tb_: (lambda e: e.tensor_tensor(out=xlo[:, tb_, :], in0=xnt, in1=xhi[:, tb_, :], op=ALU.subtract)))(tb),
                              r=["xnt", ("xhi", tb)], w=[("xlo", tb)])
                    for (sl, r0) in hp:
                        P.dma(tmpA, self.x[r0:r0 + 128, c0:c0 + 512], w=[("rtmp", 0)])
                        P.dve((lambda sl_: (lambda e: e.scalar_tensor_tensor(out=tmpA, in0=tmpA, scalar=rstd_h[:, sl_:sl_ + 1], in1=bcC, op0=ALU.mult, op1=ALU.mult)))(sl),
                              r=[("rtmp", 0), "rstd_h", "bcC"], w=[("rtmp", 0)])
                        P.dve((lambda sl_: (lambda e: e.tensor_copy(out=hhi[:, sl_, :], in_=tmpA)))(sl), r=[("rtmp", 0)], w=[("hhi", sl)])
                        P.dve((lambda sl_: (lambda e: e.tensor_tensor(out=hlo[:, sl_, :], in0=tmpA, in1=hhi[:, sl_, :], op=ALU.subtract)))(sl),
                              r=[("rtmp", 0), ("hhi", sl)], w=[("hlo", sl)])
                    for kc in range(4):
                        bank = dps_banks[dps_i % 2]
                        dps_i += 1
                        pm = self.ps[bank]
                        ks = slice(kc * 128, (kc + 1) * 128)
                        for tb in range(4):
                            gb = ti * 4 + tb
                            terms = []
                            bself = 3 if gb == 0 else (4 if gb == S // 128 - 1 else 0)
                            bs_ = bands[:, g * 5 + bself, :]
                            terms.append((xhi[:, tb, ks], bs_, [("xhi", tb)]))
                            terms.append((xlo[:, tb, ks], bs_, [("xlo", tb)]))
                            bp_ = bands[:, g * 5 + 1, :]
                            if tb > 0:
                                terms.append((xhi[:, tb - 1, ks], bp_, [("xhi", tb - 1)]))
                                terms.append((xlo[:, tb - 1, ks], bp_, [("xlo", tb - 1)]))
                            elif 0 in hslots:
                                terms.append((hhi[:, 0, ks], bp_, [("hhi", 0)]))
                                terms.append((hlo[:, 0, ks], bp_, [("hlo", 0)]))
                            bn_ = bands[:, g * 5 + 2, :]
                            if tb < 3:
                                terms.append((xhi[:, tb + 1, ks], bn_, [("xhi", tb + 1)]))
                                terms.append((xlo[:, tb + 1, ks], bn_, [("xlo", tb + 1)]))
                            elif 1 in hslots:
                                terms.append((hhi[:, 1, ks], bn_, [("hhi", 1)]))
                                terms.append((hlo[:, 1, ks], bn_, [("hlo", 1)]))
                            for n, (l_, r_, k_) in enumerate(terms):
                                P.pe((lambda l__, r__, n_, nt_, tb_, pm_: (lambda e: e.matmul(pm_[:, tb_ * 128:(tb_ + 1) * 128], lhsT=l__, rhs=r__,
                                                                                              start=(n_ == 0), stop=(n_ == nt_ - 1))))(l_, r_, n, len(terms), tb, pm),
                                     r=k_ + ["bands"], w=[("ps", bank)])
                        P.act((lambda pm_, j_: (lambda e: e.copy(out=dT[:, j_, :], in_=pm_[:])))(pm, hh * 4 + kc),
                              r=[("ps", bank)], w=[("aT", 0, hh * 4 + kc)])
                wp = self.pool_w[g].rearrange("(kc p) n -> p kc n", p=128)
                for cb in range(2):
                    wi = self.wb_i % 2
                    self.wb_i += 1
                    P.dma(self.WB[wi], wp[:, :, cb * 512:(cb + 1) * 512], w=[("WB", wi)], q="pool")
                    for tb in range(4):
                        bank = self.next_mm_bank()
                        pm = self.ps[bank]
                        for kc in range(8):
                            P.pe((lambda kc_, tb_, wi_, pm_: (lambda e: e.matmul(pm_[:], lhsT=dT[:, kc_, tb_ * 128:(tb_ + 1) * 128], rhs=WB_[wi_][:, kc_, :],
                                                                                  start=(kc_ == 0), stop=(kc_ == 7))))(kc, tb, wi, pm),
                                 r=[("WB", wi), ("aT", 0, kc)], w=[("ps", bank)])
                        hs = self.h[:, tb, g * 1024 + cb * 512:g * 1024 + (cb + 1) * 512]
                        P.dve((lambda pm_, cb_: (lambda e: e.tensor_tensor(out=tmpA, in0=pm_[:], in1=bcB[:, cb_ * 512:(cb_ + 1) * 512], op=ALU.mult)))(pm, cb),
                              r=[("ps", bank), "bcB"], w=[("rtmp", 0)])
                        P.dve((lambda hs_: (lambda e: e.tensor_tensor(out=hs_, in0=hs_, in1=tmpA, op=ALU.add)))(hs),
                              r=[("rtmp", 0), ("h", tb)], w=[("h", tb)])
            flat = lambda ap: ap.rearrange("p a b -> p (a b)")
            for tb in range(4):
                xb_, xk_ = (self.xnb, ["xnb"]) if tb % 2 == 0 else (flat(self.aT[1]), [("aT", 1, f) for f in range(8)])
                self.norm_transpose(self.h[:, tb, :], [("h", tb)], tb, 0, xb_, rstd[:, tb:tb + 1], ("rstd", tb), xkeys=xk_,
                                    junk=flat(self.aT[0]), junk_keys=[("aT", 0, f) for f in range(8)])
            self.mlp(0, self.tiles.index(ti), len(self.tiles))
            for tb in range(4):
                P.dma(self.h1[t0 + tb * 128:t0 + (tb + 1) * 128, :], self.h[:, tb, :], r=[("h", tb)], w=[("h1", ti, tb)])


    def phase2(self):
        P = self.P
        self.hnT = self.alloc(KC * 1024, BF16, shape=(KC, 1024))
        hnT = self.hnT
        self.xnb = self.alloc(D, BF16)
        WA = [self.alloc(KC * 256, BF16, shape=(KC, 256)) for _ in range(2)]
        wv_f = [self.alloc(KC * 256) for _ in range(2)]
        WV = [w.bitcast(BF16).rearrange("p (kc j) -> p kc j", kc=KC) for w in wv_f]
        hs = [w[:, 0:D] for w in wv_f]
        wr = self.alloc(KC * 128, BF16, shape=(KC, 128))
        wup = self.alloc(2 * 2048, BF16, shape=(2, 2048))
        negb = self.alloc(32, shape=(2, 16))
        rT = [self.alloc(512, BF16) for _ in range(2)]
        st_q = [self.alloc(512) for _ in range(2)]
        st_g = [self.alloc(512) for _ in range(2)]
        st_v = [self.alloc(512, BF16) for _ in range(2)]
        st_gate = [self.alloc(512) for _ in range(2)]
        rstd = self.alloc(8)
        self.tr_banks = [4, 5]
        self.mm_banks = [0, 1, 2, 3]
        self.tr_i = 0
        self.mm_i = 0
        win = self.w_in.rearrange("(kc p) n -> p kc n", p=128)
        P.dve(lambda e: e.memset(wup, 0.0), w=["wup"])
        P.dma(wup[96:112, 0, :], self.w_up[0], w=["wup"], q="pool")
        P.dma(wup[112:128, 1, :], self.w_up[1], w=["wup"], q="pool")
        for d in range(2):
            P.dma(negb[:, d, :], self.b_up[d].rearrange("o (c p) -> p (o c)", p=128), w=["negb"], allow_slow_non_contiguous=True)
        P.dve(lambda e: e.tensor_scalar_mul(out=negb, in0=negb, scalar1=-1.0), r=["negb"], w=["negb"])
        P.dma(wr, win[:, :, PROJ - 128:PROJ], w=["wr"], q="pool")
        qscale = float(HK ** -0.5)
        wa_i = 0
        wv_i = 0
        sq_i = 0
        sg_i = 0
        sv_i = 0
        pairs = [self.tiles[i:i + 2] for i in range(0, len(self.tiles), 2)]
        hn_keys = [("hnT", t8) for t8 in range(8)]
        for pair in pairs:
            for j, ti in enumerate(pair):
                t0 = ti * T
                for tb in range(4):
                    t8 = j * 4 + tb
                    hb = hs[t8 % 2]
                    P.dma(hb, self.h1[t0 + tb * 128:t0 + (tb + 1) * 128, :], r=[("h1", ti, tb)], w=[("WV", t8 % 2)])
                    wflat = lambda ap: ap[:, 0:16, :].rearrange("p a b -> p (a b)")
                    xb_, xk_ = (self.xnb, ["xnb"]) if t8 % 2 == 0 else (wflat(WA[1]), [("WA", 1)])
                    self.norm_transpose(hb, [("WV", t8 % 2)], t8, 1, xb_, rstd[:, t8:t8 + 1], ("rstd", t8), xkeys=xk_,
                                        junk=wflat(WA[0]), junk_keys=[("WA", 0)])
            for cp in range(16):
                wi = wa_i % 2
                wa_i += 1
                P.dma(WA[wi], win[:, :, cp * 256:(cp + 1) * 256], w=[("WA", wi)], q="pool")
                for j, ti in enumerate(pair):
                    t0 = ti * T
                    for cc in range(2):
                        c = cp * 2 + cc
                        bank = self.next_mm_bank()
                        pm = self.ps[bank]
                        for kc in range(KC):
                            P.pe((lambda kc_, cc_, wi_, pm_, j_: (lambda e: e.matmul(pm_[:], lhsT=WA[wi_][:, kc_, cc_ * 128:(cc_ + 1) * 128], rhs=hnT[:, kc_, j_ * 512:(j_ + 1) * 512],
                                                                                      start=(kc_ == 0), stop=(kc_ == KC - 1))))(kc, cc, wi, pm, j),
                                 r=[("WA", wi)] + hn_keys, w=[("ps", bank)])
                        si = sq_i % 2
                        sq_i += 1
                        if c < 16:
                            P.act((lambda si_, pm_: (lambda e: e.mul(out=st_q[si_], in_=pm_[:], mul=qscale)))(si, pm), r=[("ps", bank)], w=[("st_q", si)])
                            P.dma(self.qT[c * 128:(c + 1) * 128, t0:t0 + T], st_q[si], r=[("st_q", si)], w=[("qT", c, ti)])
                        else:
                            P.act((lambda si_, pm_: (lambda e: e.copy(out=st_q[si_], in_=pm_[:])))(si, pm), r=[("ps", bank)], w=[("st_q", si)])
                            P.dma(self.kT[(c - 16) * 128:(c - 15) * 128, t0:t0 + T], st_q[si], r=[("st_q", si)], w=[("kT", c - 16, ti)])
            for j, ti in enumerate(pair):
                bank = self.next_mm_bank()
                pm = self.ps[bank]
                for kc in range(KC):
                    P.pe((lambda kc_, pm_, j_: (lambda e: e.matmul(pm_[:], lhsT=wr[:, kc_, :], rhs=hnT[:, kc_, j_ * 512:(j_ + 1) * 512], start=(kc_ == 0), stop=(kc_ == KC - 1))))(kc, pm, j),
                         r=["wr"] + hn_keys, w=[("ps", bank)])
                P.act((lambda pm_, j_: (lambda e: e.copy(out=rT[j_], in_=pm_[:])))(pm, j), r=[("ps", bank)], w=[("rT", j)])
            for j, ti in enumerate(pair):
                t0 = ti * T
                for d in range(2):
                    for c in range(16):
                        bank = self.next_mm_bank()
                        pm = self.ps[bank]
                        P.pe((lambda d_, c_, pm_, j_: (lambda e: e.matmul(pm_[:], lhsT=wup[:, d_, c_ * 128:(c_ + 1) * 128], rhs=rT[j_], start=True, stop=True)))(d, c, pm, j),
                             r=["wup", ("rT", j)], w=[("ps", bank)])
                        si = sg_i % 2
                        sg_i += 1
                        P.act((lambda d_, c_, si_, pm_: (lambda e: e.activation(out=st_g[si_], in_=pm_[:], func=AF.Exp, bias=negb[:, d_, c_:c_ + 1], scale=-1.0)))(d, c, si, pm),
                              r=[("ps", bank), "negb"], w=[("st_g", si)])
                        P.act((lambda si_: (lambda e: e.activation(out=st_g[si_], in_=st_g[si_], func=AF.Ln, bias=1.0, scale=1.0)))(si),
                              r=[("st_g", si)], w=[("st_g", si)])
                        P.dve((lambda si_: (lambda e: e.tensor_scalar_mul(out=st_g[si_], in0=st_g[si_], scalar1=-1.0 / 16.0)))(si),
                              r=[("st_g", si)], w=[("st_g", si)])
                        P.dma(self.gT[d, c * 128:(c + 1) * 128, t0:t0 + T], st_g[si], r=[("st_g", si)], w=[("gT", d, c, ti)])
            for cb in range(16):
                wi = wv_i % 2
                wv_i += 1
                f0 = 4096 + cb * 512
                P.dma(WV[wi], win[:, :, f0:f0 + 512], w=[("WV", wi)], q="pool")
                for j, ti in enumerate(pair):
                    t0 = ti * T
                    for tb in range(4):
                        t8 = j * 4 + tb
                        bank = self.next_mm_bank()
                        pm = self.ps[bank]
                        for kc in range(KC):
                            P.pe((lambda kc_, t8_, wi_, pm_: (lambda e: e.matmul(pm_[:], lhsT=hnT[:, kc_, t8_ * 128:(t8_ + 1) * 128], rhs=WV[wi_][:, kc_, :],
                                                                                  start=(kc_ == 0), stop=(kc_ == KC - 1))))(kc, t8, wi, pm),
                                 r=[("WV", wi), ("hnT", t8)], w=[("ps", bank)])
                        si = sv_i % 2
                        sv_i += 1
                        rows = slice(t0 + tb * 128, t0 + (tb + 1) * 128)
                        if cb < 8:
                            P.act((lambda si_, pm_: (lambda e: e.copy(out=st_v[si_], in_=pm_[:])))(si, pm), r=[("ps", bank)], w=[("st_v", si)])
                            P.dma(self.v[rows, cb * 512:(cb + 1) * 512], st_v[si], r=[("st_v", si)], w=[("v", ti, tb, cb)])
                        else:
                            P.act((lambda si_, pm_: (lambda e: e.copy(out=st_gate[si_], in_=pm_[:])))(si, pm), r=[("ps", bank)], w=[("st_gate", si)])
                            P.dma(self.gate[rows, (cb - 8) * 512:(cb - 7) * 512], st_gate[si], r=[("st_gate", si)], w=[("gate", ti, tb, cb - 8)])

    def phase3(self):
        P = self.P
        qiT = self.alloc(4 * S, BF16, shape=(4, S))
        ksT = self.alloc(4 * S, BF16, shape=(4, S))
        kstok = self.alloc(16 * 512, BF16, shape=(16, 512))
        vh = self.alloc(16 * HV, BF16, shape=(16, HV))
        R = self.alloc(4 * HV, shape=(4, HV))
        Sbf2 = [self.alloc(4 * HV, BF16, shape=(4, HV)) for _ in range(2)]
        HS = S // 2
        gt2_ = [self.alloc(HS) for _ in range(2)]
        bt_2 = [self.alloc(HS) for _ in range(2)]
        bt2_2 = [self.alloc(HS) for _ in range(2)]
        ex_2 = [self.alloc(HS) for _ in range(2)]
        qk_2 = [self.alloc(HS) for _ in range(2)]
        ex2_2 = [self.alloc(HS) for _ in range(2)]
        qk2_2 = [self.alloc(HS) for _ in range(2)]
        pp_i = 0
        A = self.alloc(64, shape=(4, 16))
        rmask = self.alloc(S)
        tri = self.alloc(256, shape=(2, 128))
        sT = [self.alloc(128, BF16) for _ in range(2)]
        ostage = [self.alloc(HV) for _ in range(2)]
        NCH = S // 128
        P.dma(tri, self.c_tri.rearrange("d j i -> j d i"), w=["tri"])
        P.dve(lambda e: e.memset(rmask, 1.0), w=["rmask"])
        P.dve(lambda e: e.memset(rmask.rearrange("p (c t) -> p c t", t=128)[:, :, 0:1], 0.0), w=["rmask"])
        v3 = lambda ap: ap.rearrange("p (c t) -> p c t", t=128)
        kv_i = 0
        all_keys = lambda name: [(name, c) for c in range(4)]
        for hd in range(NH):
            P.dma(vh, self.v.rearrange("(n p) c -> p n c", p=128)[:, :, hd * HV:(hd + 1) * HV],
                  r=[("v", ti, tb, cb) for ti in self.tiles for tb in range(4) for cb in range(8)], w=["vh"])
            for d in range(2):
                for c in range(4):
                    fr = hd * HK + c * 128
                    for hf in range(2):
                        pi = pp_i % 2
                        pp_i += 1
                        gt, bt, bt2, ex, qk, ex2, qk2 = gt2_[pi], bt_2[pi], bt2_2[pi], ex_2[pi], qk_2[pi], ex2_2[pi], qk2_2[pi]
                        kk = lambda nm: (nm, pi)
                        ts_ = slice(hf * HS, (hf + 1) * HS)
                        NH2 = NCH // 2
                        P.dma(gt, self.gT[d, fr:fr + 128, ts_], w=[kk("gt")])
                        P.dve((lambda bt_, gt_, ts__: (lambda e: e.tensor_tensor_scan(out=bt_, data0=rmask[:, ts__], data1=gt_, initial=0.0, op0=ALU.mult, op1=ALU.add)))(bt, gt, ts_),
                              r=[kk("gt"), "rmask"], w=[kk("bt")])
                        if d == 0:
                            b_ = bt
                            bkey = kk("bt")
                            tot = v3(bt)[:, :, 127]
                        else:
                            P.dve((lambda bt_, gt_: (lambda e: e.tensor_tensor(out=gt_, in0=gt_, in1=bt_, op=ALU.subtract)))(bt, gt), r=[kk("gt"), kk("bt")], w=[kk("gt")])
                            P.dve((lambda bt_, gt_, b2_: (lambda e: e.tensor_tensor(out=v3(b2_), in0=v3(gt_), in1=v3(bt_)[:, :, 127:128].to_broadcast([128, NH2, 128]), op=ALU.add)))(bt, gt, bt2),
                                  r=[kk("gt"), kk("bt")], w=[kk("bt2")])
                            b_ = bt2
                            bkey = kk("bt2")
                            tot = v3(bt2)[:, :, 0]
                        P.act((lambda b__, ex_: (lambda e: e.activation(out=ex_, in_=b__, func=AF.Exp)))(b_, ex), r=[bkey], w=[kk("ex")])
                        P.dma(qk, self.qT[fr:fr + 128, ts_], w=[kk("qk")])
                        P.pool((lambda c_, qk_, ex_, ts__: (lambda e: e.tensor_tensor(out=qiT[:, c_, ts__], in0=qk_, in1=ex_, op=ALU.mult)))(c, qk, ex, ts_),
                               r=[kk("qk"), kk("ex")], w=[("qiT", c)])
                        P.act((lambda b__, ex_: (lambda e: e.activation(out=ex_, in_=b__, func=AF.Exp, scale=-1.0)))(b_, ex2), r=[bkey], w=[kk("ex2")])
                        P.dma(qk2, self.kT[fr:fr + 128, ts_], w=[kk("qk2")])
                        P.pool((lambda c_, qk_, ex_, ts__: (lambda e: e.tensor_tensor(out=ksT[:, c_, ts__], in0=qk_, in1=ex_, op=ALU.mult)))(c, qk2, ex2, ts_),
                               r=[kk("qk2"), kk("ex2")], w=[("ksT", c)])
                        P.act((lambda c_, tot_, hf_: (lambda e: e.activation(out=A[:, c_, hf_ * 8:(hf_ + 1) * 8], in_=tot_, func=AF.Exp)))(c, tot, hf), r=[bkey], w=[("A", c)])
                for n in range(NCH):
                    pb = 6 + (n % 2)
                    pt = self.ps[pb][:].bitcast(BF16)
                    for c in range(4):
                        P.pe((lambda c_, n_, pt_: (lambda e: e.transpose(pt_[:, c_ * 128:(c_ + 1) * 128], ksT[:, c_, n_ * 128:(n_ + 1) * 128], self.identb)))(c, n, pt),
                             r=[("ksT", c), "identb"], w=[("ps", pb)])
                    P.act((lambda n_, pt_: (lambda e: e.copy(out=kstok[:, n_, :], in_=pt_[:, 0:512])))(n, pt), r=[("ps", pb)], w=[("kstok", n)])
                order = list(range(NCH)) if d == 0 else list(range(NCH - 1, -1, -1))
                kvb = [2, 3, 4, 5]
                psc = self.ps[7]

                def emit_sc(step):
                    n = order[step]
                    nb = slice(n * 128, (n + 1) * 128)
                    for c in range(4):
                        P.pe((lambda c_, nb_: (lambda e: e.matmul(psc[:, 0:128], lhsT=ksT[:, c_, nb_], rhs=qiT[:, c_, nb_], start=(c_ == 0), stop=(c_ == 3))))(c, nb),
                             r=[("ksT", c), ("qiT", c)], w=[("ps", 7)])
                    sti = step % 2
                    P.dve((lambda sti_, d_: (lambda e: e.tensor_tensor(out=sT[sti_], in0=psc[:, 0:128], in1=tri[:, d_, :], op=ALU.mult)))(sti, d),
                          r=[("ps", 7), "tri"], w=[("sT", sti)])

                def emit_kv(step, idx):
                    nonlocal kv_i
                    n = order[step]
                    c, vv = idx // 2, idx % 2
                    vs = slice(vv * 512, (vv + 1) * 512)
                    bank = kvb[kv_i % 4]
                    kv_i += 1
                    pk = self.ps[bank]
                    P.pe((lambda c_, n_, pk_, vs_: (lambda e: e.matmul(pk_[:], lhsT=kstok[:, n_, c_ * 128:(c_ + 1) * 128], rhs=vh[:, n_, vs_], start=True, stop=True)))(c, n, pk, vs),
                         r=[("kstok", n), "vh"], w=[("ps", bank)])
                    return bank

                def emit_upd(step, idx, bank):
                    n = order[step]
                    c, vv = idx // 2, idx % 2
                    vs = slice(vv * 512, (vv + 1) * 512)
                    pk = self.ps[bank]
                    Snew = Sbf2[step % 2]
                    if step == 0:
                        P.dve((lambda c_, pk_, vs_: (lambda e: e.tensor_copy(out=R[:, c_, vs_], in_=pk_[:])))(c, pk, vs),
                              r=[("ps", bank)], w=[("R", c, vv)])
                    else:
                        npv = order[step - 1]
                        P.dve((lambda c_, pk_, vs_, np_: (lambda e: e.scalar_tensor_tensor(out=R[:, c_, vs_], in0=R[:, c_, vs_], scalar=A[:, c_, np_:np_ + 1], in1=pk_[:],
                                                                                          op0=ALU.mult, op1=ALU.add)))(c, pk, vs, npv),
                              r=[("ps", bank), ("R", c, vv), ("A", c)], w=[("R", c, vv)])
                    P.act((lambda c_, vs_, n_, Sn_: (lambda e: e.activation(out=Sn_[:, c_, vs_], in_=R[:, c_, vs_], func=AF.Copy, scale=A[:, c_, n_:n_ + 1])))(c, vs, n, Snew),
                          r=[("R", c, vv), ("A", c)], w=[("Sbf", step % 2, c, vv)])

                def emit_o(step):
                    n = order[step]
                    nb = slice(n * 128, (n + 1) * 128)
                    sti = step % 2
                    Sprev = Sbf2[(step + 1) % 2]
                    for vv in range(2):
                        po = self.ps[vv]
                        vs = slice(vv * 512, (vv + 1) * 512)
                        if step > 0:
                            for c in range(4):
                                P.pe((lambda c_, nb_, po_, vs_, Sp_: (lambda e: e.matmul(po_[:], lhsT=qiT[:, c_, nb_], rhs=Sp_[:, c_, vs_], start=(c_ == 0), stop=False)))(c, nb, po, vs, Sprev),
                                     r=[("qiT", c), ("Sbf", (step + 1) % 2, c, vv)], w=[("ps", vv)])
                        P.pe((lambda sti_, n_, po_, vs_, st_: (lambda e: e.matmul(po_[:], lhsT=sT[sti_], rhs=vh[:, n_, vs_], start=st_, stop=True)))(sti, n, po, vs, step == 0),
                             r=[("sT", sti), "vh"], w=[("ps", vv)])
                        P.act((lambda sti_, po_, vs_: (lambda e: e.copy(out=ostage[sti_][:, vs_], in_=po_[:])))(sti, po, vs),
                              r=[("ps", vv)], w=[("ostage", sti)])
                    P.dma(self.ofb[d, n * 128:(n + 1) * 128, hd * HV:(hd + 1) * HV], ostage[sti], r=[("ostage", sti)], w=[("ofb", d, n, hd)])

                emit_sc(0)
                pre = [emit_kv(0, i) for i in range(4)]
                for step in range(NCH):
                    if step < NCH - 1:
                        banks = list(pre)
                        for i in range(4):
                            emit_upd(step, i, banks[i])
                            banks.append(emit_kv(step, 4 + i))
                        emit_o(step)
                        for i in range(4, 8):
                            emit_upd(step, i, banks[i])
                    else:
                        emit_o(step)
                    if step + 1 < NCH:
                        emit_sc(step + 1)
                        if step + 1 < NCH - 1:
                            pre = [emit_kv(step + 1, i) for i in range(4)]

    def phase4(self):
        P = self.P
        self.h = self.alloc(4 * D, shape=(4, D))
        self.hnT = self.alloc(KC * 512, BF16, shape=(KC, 512))
        self.xnb = self.alloc(D, BF16)
        self.alloc_mlp()
        hnT, WA = self.hnT, self.WA
        bcB = self.alloc(HV)
        of2 = [self.alloc(HV) for _ in range(2)]
        ob2 = [self.alloc(HV) for _ in range(2)]
        gt2 = [self.alloc(HV) for _ in range(2)]
        og2 = [self.alloc(HV, BF16) for _ in range(2)]
        pro_i = 0
        rstd = self.alloc(8)
        rstd_o = self.alloc(8)
        self.tr_banks = [4, 5]
        self.mm_banks = [0, 1, 2, 3]
        self.tr_i = 0
        self.mm_i = 0
        wout = self.w_out.rearrange("(kc p) n -> p kc n", p=128)
        for ti in self.tiles:
            t0 = ti * T
            for tb in range(4):
                P.dma(self.h[:, tb, :], self.h1[t0 + tb * 128:t0 + (tb + 1) * 128, :], r=[("h1", ti, tb)], w=[("h", tb)])
            P.dma(bcB, self.g_norm[0:1, :].partition_broadcast(128), w=["bcB"])
            for tb in range(4):
                rows = slice(t0 + tb * 128, t0 + (tb + 1) * 128)
                n = ti * 4 + tb
                for hd in range(NH):
                    cs = slice(hd * HV, (hd + 1) * HV)
                    bi = pro_i % 2
                    pro_i += 1
                    of_, ob_, gt_, og = of2[bi], ob2[bi], gt2[bi], og2[bi]
                    kof, kob, kgt, kog, krs = ("of_", bi), ("ob_", bi), ("gt_", bi), ("og", bi), ("rstd_o", bi)
                    rso = rstd_o[:, bi:bi + 1]
                    P.dma(of_, self.ofb[0, rows, cs], r=[("ofb", 0, n, hd)], w=[kof])
                    P.dma(ob_, self.ofb[1, rows, cs], r=[("ofb", 1, n, hd)], w=[kob])
                    P.dma(gt_, self.gate[rows, cs], r=[("gate", ti, tb, cb) for cb in (2 * hd, 2 * hd + 1)], w=[kgt])
                    P.dve((lambda of__, ob__: (lambda e: e.tensor_tensor(out=of__, in0=of__, in1=ob__, op=ALU.add)))(of_, ob_), r=[kof, kob], w=[kof])
                    self.rstd_of(of_, rso, ob_, [kof], HV, krs, junk_key=kob)
                    P.act((lambda gt__: (lambda e: e.activation(out=gt__, in_=gt__, func=AF.Silu)))(gt_), r=[kgt], w=[kgt])
                    P.dve((lambda of__, rso_: (lambda e: e.scalar_tensor_tensor(out=of__, in0=of__, scalar=rso_, in1=bcB, op0=ALU.mult, op1=ALU.mult)))(of_, rso),
                          r=[kof, krs, "bcB"], w=[kof])
                    P.dve((lambda og_, of__, gt__: (lambda e: e.tensor_tensor(out=og_, in0=of__, in1=gt__, op=ALU.mult)))(og, of_, gt_), r=[kof, kgt], w=[kog])
                    self.transpose_into_hnT(og, kog, tb, None, kc0=hd * 8, nkc=8)
            for ob in range(16):
                wi = self.wa_i % 2
                self.wa_i += 1
                P.dma(self.WA[wi], wout[:, :, ob * 256:(ob + 1) * 256], w=[("WA", wi)], q="pool")
                for tb in range(4):
                    bank = self.next_mm_bank()
                    pm = self.ps[bank]
                    for kc in range(KC):
                        P.pe((lambda kc_, tb_, wi_, pm_: (lambda e: e.matmul(pm_[:, 0:256], lhsT=hnT[:, kc_, tb_ * 128:(tb_ + 1) * 128], rhs=WA[wi_][:, kc_, :],
                                                                              start=(kc_ == 0), stop=(kc_ == KC - 1))))(kc, tb, wi, pm),
                             r=[("WA", wi), ("hnT", tb)], w=[("ps", bank)])
                    hsl = self.h[:, tb, ob * 256:(ob + 1) * 256]
                    P.dve((lambda hs_, pm_: (lambda e: e.tensor_tensor(out=hs_, in0=hs_, in1=pm_[:, 0:256], op=ALU.add)))(hsl, pm),
                          r=[("ps", bank), ("h", tb)], w=[("h", tb)])
            flat = lambda ap: ap.rearrange("p a b -> p (a b)")
            for tb in range(4):
                xb_, xk_ = (self.xnb, ["xnb"]) if tb % 2 == 0 else (flat(self.aT[1]), [("aT", 1, f) for f in range(8)])
                self.norm_transpose(self.h[:, tb, :], [("h", tb)], tb, 2, xb_, rstd[:, tb:tb + 1], ("rstd", tb), xkeys=xk_,
                                    junk=flat(self.aT[0]), junk_keys=[("aT", 0, f) for f in range(8)])
            self.mlp(1, self.tiles.index(ti), len(self.tiles))
            for tb in range(4):
                self.rstd_of(self.h[:, tb, :], rstd[:, tb:tb + 1], self.xnb, [("h", tb)], D, ("rstd", tb))
            for pc in range(4):
                P.dma(bcB, self.norm_final[0:1, pc * 1024:(pc + 1) * 1024].partition_broadcast(128), w=["bcB"])
                for tb in range(4):
                    hsl = self.h[:, tb, pc * 1024:(pc + 1) * 1024]
                    P.dve((lambda hs_, tb_: (lambda e: e.scalar_tensor_tensor(out=hs_, in0=hs_, scalar=rstd[:, tb_:tb_ + 1], in1=bcB, op0=ALU.mult, op1=ALU.mult)))(hsl, tb),
                          r=[("h", tb), ("rstd", tb), "bcB"], w=[("h", tb)])
            for tb in range(4):
                P.dma(self.out[t0 + tb * 128:t0 + (tb + 1) * 128, :], self.h[:, tb, :], r=[("h", tb)], w=[("out", ti, tb)])


def build_program(**kw):
    b = Builder(**kw)
    return b.build(), b


def host_inputs(names, batch, inp, consts):
    m = {
        "x": lambda: inp["x"][batch],
        "norm_mix": lambda: inp["norm_mix"],
        "norm_mlp": lambda: inp["norm_mlp"],
        "norm_final": lambda: inp["norm_final"].reshape(1, D),
        "pool_w": lambda: inp["pool_w"][0],
        "pool_scale": lambda: inp["pool_scale"].reshape(1, D),
        "gla_w_in": lambda: inp["gla_w_in"][0],
        "gla_w_up_f": lambda: inp["gla_w_up_f"][0],
        "gla_w_up_b": lambda: inp["gla_w_up_b"][0],
        "gla_b_up_f": lambda: inp["gla_b_up_f"].reshape(1, 2048),
        "gla_b_up_b": lambda: inp["gla_b_up_b"].reshape(1, 2048),
        "gla_g_norm": lambda: inp["gla_g_norm"].reshape(1, HV),
        "gla_w_out": lambda: inp["gla_w_out"][0],
        "mlp_w_in": lambda: inp["mlp_w_in"],
        "mlp_w_out": lambda: inp["mlp_w_out"],
    }
    out = {}
    for n in names:
        if n in consts:
            out[n] = consts[n]
        elif n in m:
            out[n] = np.ascontiguousarray(m[n]())
        else:
            out[n] = np.ascontiguousarray(inp[n])
    return out


def kernel(**inputs):
    inputs = {k: np.asarray(v) for k, v in inputs.items()}
    nc, b = build_program()
    consts = make_consts()
    n = 8
    in_maps = [host_inputs(b.in_names, i, inputs, consts) for i in range(n)]
    res = run_bass_kernel_spmd(nc, in_maps, core_ids=list(range(n)))
    return np.stack([np.asarray(r["out"]) for r in res.results]).astype(np.float32)
```
